# Optimizing a Trainium2 kernel written in Bass

```python
import math
import jax, jax.numpy as jnp
from jax import lax
import numpy as np

D_MODEL = 1024
BATCH = 2
SEQ = 8192
DEPTH = 1

D_MIX = D_MODEL
CONV_CH = D_MIX // 2
CONV_GROUPS = 8
N_HEADS = 8
HEAD_DIM = 64
ATT_CH = N_HEADS * HEAD_DIM
CONV_WIDTH = 31
GRID_W = 64
MAX_KH = 8
KW = 16
EPS = 1e-6
PROJ_OUT = 3 * CONV_CH + 4 * ATT_CH

kernel_name = "hybrid_conformer_conv_natten_block"


def rms_norm(x, g):
    x32 = x.astype(jnp.float32)
    y = x32 * lax.rsqrt(jnp.mean(x32 * x32, axis=-1, keepdims=True) + EPS)
    return (y * g.astype(jnp.float32)).astype(x.dtype)


def layer_norm(x, g, b):
    x32 = x.astype(jnp.float32)
    mu = jnp.mean(x32, axis=-1, keepdims=True)
    var = jnp.mean(jnp.square(x32 - mu), axis=-1, keepdims=True)
    y = (x32 - mu) * lax.rsqrt(var + EPS)
    return (y * g.astype(jnp.float32) + b.astype(jnp.float32)).astype(x.dtype)


def grouped_rms_norm(y, g, n_groups):
    B, T, C = y.shape
    y32 = y.astype(jnp.float32).reshape(B, T, n_groups, C // n_groups)
    y32 = y32 * lax.rsqrt(jnp.mean(y32 * y32, axis=-1, keepdims=True) + EPS)
    return (y32.reshape(B, T, C) * g.astype(jnp.float32)).astype(y.dtype)


def conformer_conv(glu_a, glu_b, dw_w, dw_b, cln_g, cln_b, pw_w, pw_b):
    u = glu_a * jax.nn.sigmoid(glu_b)
    pad = CONV_WIDTH // 2
    u = lax.conv_general_dilated(
        u, dw_w[:, None, :].astype(u.dtype), window_strides=(1,),
        padding=[(pad, pad)], dimension_numbers=("NWC", "WIO", "NWC"),
        feature_group_count=u.shape[-1]) + dw_b
    u = layer_norm(u, cln_g, cln_b)
    u = jax.nn.silu(u)
    return u @ pw_w + pw_b


def neighbourhood_attention(q, k, v, rpb):
    B, T, _ = q.shape
    rows = T // GRID_W
    kh = min(MAX_KH, rows)

    def to_grid(t):
        return t.reshape(B, rows, GRID_W, N_HEADS, HEAD_DIM).transpose(0, 3, 1, 2, 4)

    qg, kg, vg = to_grid(q), to_grid(k), to_grid(v)
    scale = HEAD_DIM ** -0.5

    row_ids = jnp.arange(rows)
    row_start = jnp.clip(row_ids - kh // 2, 0, rows - kh)
    col_ids = jnp.arange(GRID_W)
    col_start = jnp.clip(col_ids - KW // 2, 0, GRID_W - KW)
    col_idx = col_start[:, None] + jnp.arange(KW)[None, :]
    col_rel = col_idx - col_ids[:, None] + (KW - 1)

    def one_row(args):
        r, rs = args
        q_row = lax.dynamic_index_in_dim(qg, r, axis=2, keepdims=False)
        k_rows = lax.dynamic_slice_in_dim(kg, rs, kh, axis=2)
        v_rows = lax.dynamic_slice_in_dim(vg, rs, kh, axis=2)
        k_win = k_rows[:, :, :, col_idx, :]
        v_win = v_rows[:, :, :, col_idx, :]
        s = jnp.einsum("bhwd,bhiwjd->bhwij", q_row, k_win).astype(jnp.float32) * scale
        row_rel = rs + jnp.arange(kh) - r + (MAX_KH - 1)
        bias = rpb[:, row_rel[:, None, None], col_rel[None, :, :]]
        s = s + bias.transpose(0, 2, 1, 3).astype(jnp.float32)[None]
        p = jax.nn.softmax(s.reshape(B, N_HEADS, GRID_W, kh * KW), axis=-1)
        p = p.reshape(B, N_HEADS, GRID_W, kh, KW).astype(v.dtype)
        return jnp.einsum("bhwij,bhiwjd->bhwd", p, v_win)

    out = lax.map(one_row, (row_ids, row_start))
    return out.transpose(1, 0, 3, 2, 4).reshape(B, T, ATT_CH)


def setup_inputs(seed: int = 0) -> dict:
    key = jax.random.key(seed)
    ks = jax.random.split(key, 16)
    f32 = jnp.float32
    nrm = lambda k, s: jax.random.normal(k, s, f32)
    return {
        "x": nrm(ks[0], (BATCH, SEQ, D_MODEL)),
        "ln_g": 1.0 + 0.01 * nrm(ks[1], (DEPTH, D_MODEL)),
        "w_in": nrm(ks[2], (DEPTH, D_MODEL, PROJ_OUT)) * D_MODEL ** -0.5,
        "b_in": 0.01 * nrm(ks[3], (DEPTH, PROJ_OUT)),
        "dw_w": nrm(ks[4], (DEPTH, CONV_WIDTH, CONV_CH)) * CONV_WIDTH ** -0.5,
        "dw_b": 0.01 * nrm(ks[5], (DEPTH, CONV_CH)),
        "cln_g": 1.0 + 0.01 * nrm(ks[6], (DEPTH, CONV_CH)),
        "cln_b": 0.01 * nrm(ks[7], (DEPTH, CONV_CH)),
        "pw_w": nrm(ks[8], (DEPTH, CONV_CH, CONV_CH)) * CONV_CH ** -0.5,
        "pw_b": 0.01 * nrm(ks[9], (DEPTH, CONV_CH)),
        "rpb": 0.02 * nrm(ks[10], (DEPTH, N_HEADS, 2 * MAX_KH - 1, 2 * KW - 1)),
        "gn_conv_g": 1.0 + 0.01 * nrm(ks[11], (DEPTH, CONV_CH)),
        "gn_att_g": 1.0 + 0.01 * nrm(ks[12], (DEPTH, ATT_CH)),
        "w_out": nrm(ks[13], (DEPTH, D_MIX, D_MODEL)) * D_MIX ** -0.5,
        "final_g": 1.0 + 0.01 * nrm(ks[14], (D_MODEL,)),
    }


def reference(x, ln_g, w_in, b_in, dw_w, dw_b, cln_g, cln_b, pw_w, pw_b, rpb,
              gn_conv_g, gn_att_g, w_out, final_g):
    h = x
    cuts = [CONV_CH, 2 * CONV_CH, 3 * CONV_CH, 3 * CONV_CH + ATT_CH,
            3 * CONV_CH + 2 * ATT_CH, 3 * CONV_CH + 3 * ATT_CH]
    for l in range(DEPTH):
        hn = rms_norm(h, ln_g[l])
        proj = hn @ w_in[l] + b_in[l]
        glu_a, glu_b, z_conv, q, k, v, z_att = jnp.split(proj, cuts, axis=-1)
        y_conv = conformer_conv(glu_a, glu_b, dw_w[l], dw_b[l], cln_g[l], cln_b[l],
                                pw_w[l], pw_b[l])
        y_att = neighbourhood_attention(q, k, v, rpb[l])
        y_conv = grouped_rms_norm(y_conv, gn_conv_g[l], CONV_GROUPS) * jax.nn.silu(z_conv)
        y_att = grouped_rms_norm(y_att, gn_att_g[l], N_HEADS) * jax.nn.silu(z_att)
        y = jnp.concatenate([y_conv, y_att], axis=-1)
        h = h + y @ w_out[l]
    return rms_norm(h, final_g)
```

```python
import os
from contextlib import ExitStack

import numpy as np
import ml_dtypes

import concourse.bass as bass
import concourse.mybir as mybir
from concourse.bass_utils import run_bass_kernel_spmd

F32 = mybir.dt.float32
BF16 = mybir.dt.bfloat16
AF = mybir.ActivationFunctionType
ALU = mybir.AluOpType

D = 1024
SEQ = 8192
NCORE = 8
TM = 2048
HALO = 256
NT = TM + 2 * HALO
NTILE = NT // 128
PROJ = 3584
EPS = 1e-6
UW = 520
NEG = -30000.0

ENGS = ["pe", "act", "dve", "pool", "sp"]


class Sched:
    def __init__(self):
        self.ops = {e: [] for e in ENGS}
        self.cnt = {e: 0 for e in ENGS}
        self.dcnt = {}
        self.pool_fifo = []
        self.max_swdge = 6

    def op(self, eng, fn, deps=(), sig=True):
        ev = None
        if sig:
            self.cnt[eng] += 1
            ev = (eng, self.cnt[eng])
        self.ops[eng].append((fn, [d for d in deps if d is not None], sig, None))
        return ev

    def dma(self, eng, fn, deps, sem):
        self.dcnt[sem] = self.dcnt.get(sem, 0) + 16
        ev = ("d:" + sem, self.dcnt[sem])
        deps = [d for d in deps if d is not None]
        if eng == "pool":
            if len(self.pool_fifo) >= self.max_swdge:
                deps.append(self.pool_fifo.pop(0))
            self.pool_fifo.append(ev)
        self.ops[eng].append((fn, deps, False, "d:" + sem))
        return ev

    def wait(self, eng, deps):
        self.ops[eng].append((None, [d for d in deps if d is not None], False, None))

    def emit(self, nc, stack):
        names = list(ENGS) + ["d:" + s for s in self.dcnt]
        sems = {}
        for n in names:
            sems[n] = stack.enter_context(nc.semaphore("s_" + n.replace(":", "_")))
        block = stack.enter_context(nc.Block())
        ops = self.ops

        def run(engname, engine):
            seen = {}
            for fn, deps, sig, dsem in ops[engname]:
                need = {}
                for (src, v) in deps:
                    if v > seen.get(src, 0):
                        need[src] = max(need.get(src, 0), v)
                for src, v in need.items():
                    engine.wait_ge(sems[src], v)
                    seen[src] = v
                if fn is None:
                    continue
                ins = fn(engine)
                if sig:
                    ins.then_inc(sems[engname], 1)
                elif dsem is not None:
                    ins.then_inc(sems[dsem], 16)

        @block.tensor
        def _(e):
            run("pe", e)

        @block.scalar
        def _(e):
            run("act", e)

        @block.vector
        def _(e):
            run("dve", e)

        @block.gpsimd
        def _(e):
            run("pool", e)

        @block.sync
        def _(e):
            run("sp", e)


VC = {}
_o = 0
for _n, _w in [("bua", 16), ("bub", 16), ("bzc", 4), ("bq", 4), ("bk", 4), ("bza", 4),
               ("dwb", 16), ("clng", 16), ("clnb", 16), ("pwb", 4), ("gcg", 4), ("gag", 4)]:
    VC[_n] = (_o, _w)
    _o += _w
NVEC = _o

DEBUG = bool(int(os.environ.get("MK_DEBUG", "0")))


def build_nc(debug=False, stage=99):
    nc = bass.Bass("TRN2", target_bir_lowering=False)
    dt = nc.dram_tensor
    x_d = dt("x_ext", [NT, D], F32, kind="ExternalInput").ap()
    win_d = dt("w_in", [D, PROJ], F32, kind="ExternalInput").ap()
    wout_d = dt("w_out", [D, D], F32, kind="ExternalInput").ap()
    pwr_d = dt("pw_rep", [128, 16 * 512], F32, kind="ExternalInput").ap()
    wc_d = dt("wconv", [128, 144 * 128], F32, kind="ExternalInput").ap()
    cb_d = dt("constb", [128, 3 * 128], BF16, kind="ExternalInput").ap()
    vec_d = dt("vecs", [128, NVEC], F32, kind="ExternalInput").ap()
    bv_d = dt("bv_b", [128, 512], F32, kind="ExternalInput").ap()
    lng_d = dt("lng_b", [128, D], F32, kind="ExternalInput").ap()
    fg_d = dt("fg_b", [128, D], F32, kind="ExternalInput").ap()
    ttx_d = dt("ttxb", [128, 8 * 896], F32, kind="ExternalInput").ap()
    rv_d = dt("rv", [128, 8 * 24], F32, kind="ExternalInput").ap()
    uv_d = dt("uvalid", [128, 8], F32, kind="ExternalInput").ap()
    rw_d = dt("rwin", [128, 896], F32, kind="ExternalInput").ap()
    out_d = dt("out", [TM, D], F32, kind="ExternalOutput").ap()
    S = Sched()
    with ExitStack() as st:
        cur = [(nc.sbuf_base + 63) // 64 * 64]
        top = nc.sbuf_top
        OFF = {}

        def alloc(name, shape, dtype, at=None):
            nbytes = int(np.prod(shape[1:])) * (2 if dtype == BF16 else 4)
            nbytes = (nbytes + 63) // 64 * 64
            if at is None:
                off = cur[0]
                cur[0] += nbytes
            else:
                off = at
            assert off + nbytes <= top, (name, off, nbytes, top)
            t = nc.alloc_sbuf_tensor_at(name, list(shape), dtype, offset=off)
            OFF[name] = off
            return t

        hnT = alloc("hnT", [128, 8, NT], BF16)
        regY = cur[0]
        hnTp = alloc("hnTp", [128, 8, 4, UW], BF16)
        yT = alloc("yT", [128, 8, TM], BF16, at=regY)
        cst = alloc("cst", [128, 3, 128], BF16)
        ones64 = alloc("ones64", [128, 64], BF16)
        vecs = alloc("vecs", [128, NVEC], F32)
        bq8 = alloc("bq8", [128, 4], F32)
        bvb = alloc("bvb", [128, 512], F32)
        lgfg = alloc("lgfg", [128, D], F32)
        smalls = alloc("smalls", [128, 16], F32)
        uvt = alloc("uvt", [128, 8], F32)
        ssA = alloc("ssA", [128, NTILE], F32)
        rsA = alloc("rsA", [128, NTILE], F32)
        ssD = alloc("ssD", [128, 16], F32)
        rsD = alloc("rsD", [128, 16], F32)
        NWS = 4
        wsl = [alloc(f"wsl{i}", [128, 8, 128], BF16) for i in range(NWS)]
        regS = cur[0]
        u_ph = alloc("u_ph", [128, 16, UW], BF16)
        zc = alloc("zc", [128, 4, TM], BF16, at=OFF["u_ph"])
        wcv = alloc("wcv", [128, 144, 128], BF16)
        cvs = alloc("cvs", [128, 16, 512], BF16)
        sqs = [alloc(f"sqs{i}", [128, 512], BF16) for i in range(2)]
        pwr = alloc("pwr", [128, 16, 512], BF16)
        sgm = [alloc(f"sgm{i}", [128, 260], F32) for i in range(2)]
        f32t = [alloc(f"f32t{i}", [128, 512], F32) for i in range(8)]
        conv_end = cur[0]
        cur[0] = regS
        ttx = alloc("ttx", [128, 8, 896], BF16)
        rvt = alloc("rvt", [128, 8, 24], BF16)
        rwt = alloc("rwt", [128, 896], BF16)
        assert cur[0] <= OFF["wcv"]
        cur[0] = OFF["wcv"]
        vp = alloc("vp", [128, NTILE, 4, 128], BF16)
        wv = alloc("wv", [128, 8, 512], BF16)
        kT = alloc("kT", [128, NT], BF16)
        osq = [alloc(f"osq{i}", [128, 256], BF16) for i in range(3)]
        ess = [alloc(f"ess{i}", [128, 256], BF16) for i in range(3)]
        assert cur[0] <= OFF["cvs"], (cur[0], OFF["cvs"])
        cur[0] = OFF["cvs"]
        qzA = alloc("qzA", [128, TM], BF16)
        qzB = alloc("qzB", [128, TM], BF16)
        za = alloc("za", [128, TM], BF16)
        pT = [alloc(f"pT{i}", [128, 1536], BF16) for i in range(4)]
        ttw = alloc("ttw", [128, 2, 896], BF16)
        rvf = alloc("rvf", [128, 2, 1536], BF16)
        oS = alloc("oS", [128, TM], F32)
        stS = alloc("stS", [128, TM], F32)
        wo = alloc("wo", [128, 8, D], BF16)
        att_end = cur[0]
        pa0 = OFF["cvs"]
        NXS = 8
        xts = [alloc(f"xt{i}", [128, D], F32, at=OFF["wcv"] + i * 4096) for i in range(NXS)]
        xnb = [alloc(f"xnb{i}", [128, D], BF16, at=pa0 + i * 2048) for i in range(3)]
        junkA = alloc("junkA", [128, D], BF16, at=OFF["f32t0"])
        xrs = [alloc(f"xr{i}", [128, D], F32, at=OFF["vp"] + i * 4096) for i in range(3)]
        hts = [alloc(f"ht{i}", [128, D], F32, at=OFF["vp"] + 12288 + i * 4096) for i in range(3)]
        ots = [alloc(f"ot{i}", [128, D], F32, at=OFF["vp"] + 24576 + i * 4096) for i in range(2)]
        assert OFF["vp"] + 32768 <= OFF["osq0"]
        if os.environ.get("MK_MAP"):
            print("SBUF map: regS", regS, "conv_end", conv_end, "att_end", att_end, "top", top, "free", top - max(conv_end, att_end))
        assert max(conv_end, att_end) <= top, (conv_end, att_end, top)

        ps = st.enter_context(nc.psum_tensor("ps", [128, 4096], F32))

        def bank(b, w=512):
            return ps[:, b * 512:b * 512 + w]

        ident = cst[:, 0, :]
        bd32 = cst[:, 1, :]
        bd64s = cst[:, 2, :]

        def vcol(name, j=0, n=1):
            o, w = VC[name]
            return vecs[:, o + j:o + j + n]

        mhalf = smalls[:, 0:1]
        epsT = smalls[:, 1:2]

        def prog():
            last_bank_reader = [None] * 8

            def bank_dep(b):
                return last_bank_reader[b]

            e_cst = S.dma("sp", lambda e: e.dma_start(out=cst[:].rearrange("p a b -> p (a b)"), in_=cb_d), [], "c0")
            e_vec = S.dma("sp", lambda e: e.dma_start(out=vecs[:], in_=vec_d), [], "c0")
            e_lng = S.dma("sp", lambda e: e.dma_start(out=lgfg[:], in_=lng_d), [], "c0")
            e_bv = S.dma("sp", lambda e: e.dma_start(out=bvb[:], in_=bv_d), [], "c0")
            e_uv = S.dma("sp", lambda e: e.dma_start(out=uvt[:], in_=uv_d), [], "c0")
            C0 = e_uv
            e_m0 = S.op("pool", lambda e: e.memset(smalls[:], 0.0), [])
            e_m1 = S.op("pool", lambda e: e.memset(smalls[:, 0:1], -0.5), [e_m0])
            e_m2 = S.op("pool", lambda e: e.memset(smalls[:, 1:2], EPS), [e_m1])
            e_m3 = S.op("pool", lambda e: e.memset(ones64[:], 1.0), [e_m2])
            e_m4 = S.op("pool", lambda e: e.memset(ssA[:], 0.0), [e_m3])
            e_m5 = S.op("pool", lambda e: e.memset(ssD[:], 0.0), [e_m4])
            e_sm = e_m5
            e_bq8 = S.op("dve", lambda e: e.tensor_scalar(out=bq8[:], in0=vcol("bq", 0, 4), scalar1=0.125, scalar2=None, op0=ALU.mult), [C0])

            wslot_free = [None] * NWS
            wcount = [0]

            def load_wchunk(col0):
                i = wcount[0] % NWS
                wcount[0] += 1
                ev = S.dma("pool", lambda e, i=i, col0=col0: e.dma_start(
                    out=wsl[i][:], in_=win_d[:, col0:col0 + 128].rearrange("(k p) c -> p k c", p=128)),
                    [wslot_free[i]], f"w{i}")
                return i, ev

            chunk_cols = []
            for hh_ in range(2):
                for cc in range(4):
                    chunk_cols += [512 + cc * 128, cc * 128]
            chunk_cols += [1024 + cc * 128 for cc in range(4)]
            for pp in range(4):
                chunk_cols += [1536 + pp * 128, 2048 + pp * 128, 3072 + pp * 128]
            pending = []
            nxt = [0]

            in_use = [0]

            def prefetch():
                while len(pending) + in_use[0] < NWS and nxt[0] < len(chunk_cols):
                    pending.append(load_wchunk(chunk_cols[nxt[0]]))
                    nxt[0] += 1

            def take_chunk():
                if not pending:
                    prefetch()
                slot, ev = pending.pop(0)
                in_use[0] += 1
                return slot, ev

            def release_chunk(slot, pe_ev):
                wslot_free[slot] = pe_ev
                in_use[0] -= 1
                prefetch()

            SKIP = os.environ.get("MK_SKIP", "").split(",")
            if "wpre" not in SKIP:
                prefetch()
            def big_dma(q, dst2d, src2d, deps, sem, step=2048):
                n = dst2d.shape[1]
                evs = []
                for i_, c0 in enumerate(range(0, n, step)):
                    c1 = min(n, c0 + step)
                    evs.append(S.dma(q, lambda e, c0=c0, c1=c1: e.dma_start(out=dst2d[:, c0:c1], in_=src2d[:, c0:c1]), deps, f"{sem}{i_}"))
                return evs

            rot = [[2, 3, 4, 5, 6, 7]]
            rpos = [0]

            def next_bank():
                b = rot[0][rpos[0] % len(rot[0])]
                rpos[0] += 1
                return b

            e_e2 = {}
            u_evs = []
            b1_state = {"mm": 0, "ev": 0, "chunks": None, "info": {}}

            def b1_mm():
                n = b1_state["mm"]
                hh, rem = divmod(n, 16)
                cc, gi = divmod(rem, 4)
                G = cc * 4 + gi
                n0 = hh * 260
                hdep = [e_e2[9]] if hh == 0 else [e_e2[18]]
                if gi == 0:
                    sb_, eb_ = take_chunk()
                    sa_, ea_ = take_chunk()
                    b1_state["chunks"] = (sb_, eb_, sa_, ea_)
                sb_, eb_, sa_, ea_ = b1_state["chunks"]
                bb = next_bank()
                ba = next_bank()
                evs = {}
                for (bk_, sl_, ew_) in ((bb, sb_, eb_), (ba, sa_, ea_)):
                    evp = None
                    for k in range(8):
                        for g in range(4):
                            last = (k == 7 and g == 3)
                            evp = S.op("pe", lambda e, bk_=bk_, sl_=sl_, k=k, g=g, gi=gi, n0=n0: e.matmul(
                                ps[32 * g:32 * g + 32, bk_ * 512:bk_ * 512 + 260],
                                lhsT=wsl[sl_][:, k, gi * 32:gi * 32 + 32],
                                rhs=hnTp[:, k, g, n0:n0 + 260],
                                start=(k == 0), stop=(k == 7), tile_position=(0, 32 * g)),
                                [ew_, bank_dep(bk_)] + hdep, sig=last)
                    evs[bk_] = evp
                if gi == 3:
                    release_chunk(sb_, evs[ba])
                    release_chunk(sa_, evs[ba])
                b1_state["info"][n] = (G, n0, bb, ba, evs[bb], evs[ba])
                b1_state["mm"] = n + 1

            def b1_ev():
                n = b1_state["ev"]
                G, n0, bb, ba, ev_b, ev_a = b1_state["info"].pop(n)
                si = n % 2
                e_sg = S.op("act", lambda e, bb=bb, si=si, G=G: e.activation(
                    out=sgm[si][:], in_=ps[:, bb * 512:bb * 512 + 260], func=AF.Sigmoid, bias=vcol("bub", G), scale=1.0),
                    [ev_b, C0] + ([u_evs[-2]] if len(u_evs) >= 2 else []))
                last_bank_reader[bb] = e_sg
                e_u = S.op("dve", lambda e, ba=ba, si=si, G=G, n0=n0: e.scalar_tensor_tensor(
                    out=u_ph[:, G, n0:n0 + 260], in0=ps[:, ba * 512:ba * 512 + 260], scalar=vcol("bua", G),
                    in1=sgm[si][:], op0=ALU.add, op1=ALU.mult), [ev_a, e_sg, C0])
                last_bank_reader[ba] = e_u
                u_evs.append(e_u)
                b1_state["ev"] = n + 1

            def b1_step(limit, skew):
                if b1_state["mm"] < limit:
                    while b1_state["mm"] - b1_state["ev"] > skew:
                        b1_ev()
                    b1_mm()
                    if b1_state["mm"] - b1_state["ev"] > skew:
                        b1_ev()
                    return True
                return False

            NA = NTILE
            junkP = junkA[:]
            x_free = [None] * NXS
            xn_free = [None] * 3
            e_xl = [None] * NA
            e_pow = [None] * NA
            e_xnT = [None] * NA
            e_tp = [None] * NA
            e_e1 = [None] * NA
            evac_evs = []
            tp_bank = [0, 1]

            sq_prev = [None]

            def a_load(T):
                xs = T % NXS
                e_xl[T] = S.dma("sp", lambda e, T=T, xs=xs: e.dma_start(out=xts[xs][:], in_=x_d[T * 128:(T + 1) * 128, :]),
                                [x_free[xs]], f"x{xs}")

            def a_stat(T):
                xs = T % NXS
                e_sq = S.op("act", lambda e, T=T, xs=xs: e.activation(out=junkP, in_=xts[xs][:], func=AF.Square,
                                                                        accum_out=ssA[:, T:T + 1]), [e_xl[T], e_sm, sq_prev[0]])
                sq_prev[0] = e_sq
                e_r1 = S.op("dve", lambda e, T=T: e.tensor_scalar(out=rsA[:, T:T + 1], in0=ssA[:, T:T + 1], scalar1=1.0 / D,
                                                                   scalar2=EPS, op0=ALU.mult, op1=ALU.add), [e_sq])
                e_pow[T] = S.op("pool", lambda e, T=T: e.tensor_tensor(out=rsA[:, T:T + 1], in0=rsA[:, T:T + 1], in1=mhalf,
                                                                        op=ALU.pow), [e_r1, e_sm])

            def a_norm(T):
                xs = T % NXS
                nb = T % 3
                e_xnT[T] = S.op("dve", lambda e, T=T, xs=xs, nb=nb: e.scalar_tensor_tensor(
                    out=xnb[nb][:], in0=xts[xs][:], scalar=rsA[:, T:T + 1], in1=lgfg[:], op0=ALU.mult, op1=ALU.mult),
                    [e_pow[T], e_xl[T], C0, xn_free[nb]])
                x_free[xs] = e_xnT[T]
                b = tp_bank[T % 2]
                pb = bank(b).bitcast(BF16)
                ev = None
                for k in range(8):
                    ev = S.op("pe", lambda e, k=k, nb=nb, pb=pb: e.transpose(out=pb[:, k * 128:(k + 1) * 128],
                                                                             in_=xnb[nb][:, k * 128:(k + 1) * 128], identity=ident),
                              [e_xnT[T], C0, bank_dep(b)], sig=(k == 7))
                xn_free[nb] = ev
                e_tp[T] = ev

            def a_evac1(T):
                b = tp_bank[T % 2]
                pb = bank(b).bitcast(BF16)
                e_e1[T] = S.op("act", lambda e, T=T, pb=pb: e.activation(out=hnT[:, :, T * 128:(T + 1) * 128],
                                                                           in_=pb.rearrange("p (k t) -> p k t", k=8), func=AF.Copy), [e_tp[T]])
                last_bank_reader[b] = e_e1[T]
                evac_evs.append(e_e1[T])

            def a_evac2(T):
                if not (1 <= T <= 18):
                    return
                m_lo = 28 if T == 1 else 0
                m_hi = 4 if T == 18 else 32
                mm0 = 32 * T - 60 + m_lo
                nm = m_hi - m_lo
                src = hnT[:, :, T * 128:(T + 1) * 128].rearrange("p k (m g) -> p k g m", g=4)[:, :, :, m_lo:m_hi]
                ev2 = S.op("dve", lambda e, src=src, mm0=mm0, nm=nm: e.tensor_copy(out=hnTp[:, :, :, mm0:mm0 + nm], in_=src), [e_e1[T]])
                evac_evs.append(ev2)
                e_e2[T] = ev2

            PF = 5
            for T_ in range(PF):
                a_load(T_)
            for it in range(NA + 4):
                if it + PF < NA:
                    a_load(it + PF)
                if it < NA:
                    a_stat(it)
                if 0 <= it - 1 < NA:
                    a_norm(it - 1)
                if 0 <= it - 2 < NA:
                    a_evac1(it - 2)
                if 0 <= it - 3 < NA:
                    a_evac2(it - 3)
                if it - 3 >= 9:
                    for _ in range(2 if it % 2 == 0 else 1):
                        b1_step(16, 2)
            e_xn = e_xnT[NA - 1]
            A_DONE = list(evac_evs[-3:]) + [last_bank_reader[0], last_bank_reader[1]]
            e_hn_act = [e for e in evac_evs if e[0] == "act"][-1]
            e_hn_dve = ([e for e in evac_evs if e[0] == "dve"] or [None])[-1]
            HN = [e_hn_act, e_hn_dve]
            big_jobs = []
            wc2d = wcv[:].rearrange("p a b -> p (a b)")
            pw2d = pwr[:].rearrange("p a b -> p (a b)")
            e_wc = []
            e_pw = []
            for i_, c0_ in enumerate(range(0, 144 * 128, 2048)):
                big_jobs.append(lambda i_=i_, c0_=c0_: e_wc.append(S.dma("pool", lambda e, c0_=c0_: e.dma_start(out=wc2d[:, c0_:c0_ + 2048], in_=wc_d[:, c0_:c0_ + 2048]), [e_xn, sq_prev[0]], f"wc{i_}")))
            for i_, c0_ in enumerate(range(0, 16 * 512, 2048)):
                big_jobs.append(lambda i_=i_, c0_=c0_: e_pw.append(S.dma("pool", lambda e, c0_=c0_: e.dma_start(out=pw2d[:, c0_:c0_ + 2048], in_=pwr_d[:, c0_:c0_ + 2048]), [], f"pw{i_}")))
            if stage <= 1:
                while big_jobs:
                    big_jobs.pop(0)()
            yield 1, [x for x in HN + A_DONE + (e_wc or []) + (e_pw or []) if x is not None], {"hnT": hnT[:].rearrange("p a b -> p (a b)"), "hnTp": hnTp[:].rearrange("p a b c -> p (a b c)")}


            while b1_state["ev"] < b1_state["mm"]:
                b1_ev()
            rot[0] = [2, 3, 4, 5, 6, 7, 0, 1]
            rpos[0] = 0
            while b1_step(32, 2):
                if big_jobs:
                    big_jobs.pop(0)()
            while b1_state["ev"] < 32:
                b1_ev()
            while big_jobs:
                big_jobs.pop(0)()
            uedge = bass.AP(u_ph, 0, [[16 * UW, 128], [UW, 16], [516, 2], [1, 4]])
            uvb = bass.AP(uvt, 0, [[8, 128], [0, 16], [4, 2], [1, 4]])
            e_uv2 = S.op("dve", lambda e: e.tensor_tensor(out=uedge, in0=uedge, in1=uvb, op=ALU.mult), [u_evs[-1], C0])
            U_DONE = e_uv2
            yield 2, [U_DONE], {"u": u_ph[:].rearrange("p a b -> p (a b)")}

            b_sum = next_bank()
            b_sq = next_bank()
            conv_rot = [next_bank(), next_bank()]
            sq_free = [None, None]
            cv_evs = []
            st_last = None

            def b2_stats(G):
                qi = G % 2
                e_q = S.op("dve", lambda e, G=G, qi=qi: e.tensor_tensor(out=sqs[qi][:], in0=cvs[:, G, :], in1=cvs[:, G, :], op=ALU.mult),
                           [cv_evs[G], sq_free[qi]])
                S.op("pe", lambda e, G=G: e.matmul(bank(b_sum), lhsT=bd32, rhs=cvs[:, G, :], start=(G == 0), stop=(G == 15)),
                     [cv_evs[G], C0, bank_dep(b_sum)], sig=False)
                ev = S.op("pe", lambda e, G=G, qi=qi: e.matmul(bank(b_sq), lhsT=bd32, rhs=sqs[qi][:], start=(G == 0), stop=(G == 15)),
                          [e_q, bank_dep(b_sq)], sig=True)
                sq_free[qi] = ev
                return ev

            for G in range(16):
                bc = conv_rot[G % 2]
                for di in range(9):
                    evp = S.op("pe", lambda e, bc=bc, G=G, di=di: e.matmul(
                        bank(bc), lhsT=wcv[:, G * 9 + di, :], rhs=u_ph[:, G, di:di + 512], start=(di == 0), stop=(di == 8)),
                        [U_DONE, bank_dep(bc), e_xn, e_tp[NA - 1]] + e_wc, sig=(di == 8))
                e_cv = S.op("act", lambda e, bc=bc, G=G: e.activation(out=cvs[:, G, :], in_=bank(bc), func=AF.Identity,
                                                                        bias=vcol("dwb", G), scale=1.0), [evp, C0, e_tp[NA - 1]])
                last_bank_reader[bc] = e_cv
                cv_evs.append(e_cv)
                if G >= 1:
                    st_last = b2_stats(G - 1)
            st_last = b2_stats(15)
            mean, var, lnv, rstd, mr = f32t[0], f32t[1], f32t[2], f32t[3], f32t[4]
            e1 = S.op("dve", lambda e: e.tensor_scalar(out=mean[:], in0=bank(b_sum), scalar1=1.0 / 512, scalar2=None, op0=ALU.mult), [st_last])
            e2 = S.op("dve", lambda e: e.tensor_tensor(out=var[:], in0=mean[:], in1=mean[:], op=ALU.mult), [e1])
            e3 = S.op("dve", lambda e: e.scalar_tensor_tensor(out=var[:], in0=bank(b_sq), scalar=1.0 / 512, in1=var[:],
                                                               op0=ALU.mult, op1=ALU.subtract), [e2, st_last])
            last_bank_reader[b_sum] = e3
            last_bank_reader[b_sq] = e3
            e4 = S.op("act", lambda e: e.activation(out=lnv[:], in_=var[:], func=AF.Ln, bias=epsT, scale=1.0), [e3, e_sm])
            e5 = S.op("act", lambda e: e.activation(out=rstd[:], in_=lnv[:], func=AF.Exp, scale=-0.5), [e4])
            e6 = S.op("dve", lambda e: e.tensor_tensor(out=mr[:], in0=mean[:], in1=rstd[:], op=ALU.mult), [e5, e1])
            s_evs = []
            tfree = [None, None]
            zc_evs = []
            b4 = {"chunk": None}

            def b4_unit(cc, tt):
                if tt == 0:
                    b4["chunk"] = take_chunk()
                sl_, ew_ = b4["chunk"]
                b = next_bank()
                evp = None
                for k in range(8):
                    evp = S.op("pe", lambda e, b=b, sl_=sl_, k=k, tt=tt: e.matmul(
                        bank(b), lhsT=wsl[sl_][:, k, :], rhs=hnT[:, k, HALO + tt * 512:HALO + (tt + 1) * 512],
                        start=(k == 0), stop=(k == 7)), [ew_, bank_dep(b), U_DONE] + HN, sig=(k == 7))
                ez = S.op("act", lambda e, b=b, cc=cc, tt=tt: e.activation(out=zc[:, cc, tt * 512:(tt + 1) * 512], in_=bank(b),
                                                                             func=AF.Silu, bias=vcol("bzc", cc), scale=1.0),
                          [evp, C0, st_last])
                last_bank_reader[b] = ez
                zc_evs.append(ez)
                if tt == 3:
                    release_chunk(sl_, evp)

            for G in range(16):
                ti = G % 2
                tt_ = f32t[5 + ti]
                ea = S.op("dve", lambda e, G=G, tt_=tt_: e.tensor_tensor(out=tt_[:], in0=cvs[:, G, :], in1=rstd[:], op=ALU.mult),
                          [e5, cv_evs[G], st_last, tfree[ti]])
                eb = S.op("dve", lambda e, tt_=tt_: e.tensor_tensor(out=tt_[:], in0=tt_[:], in1=mr[:], op=ALU.subtract), [ea, e6])
                es = S.op("act", lambda e, G=G, tt_=tt_: e.activation(out=cvs[:, G, :], in_=tt_[:], func=AF.Silu,
                                                                        bias=vcol("clnb", G), scale=vcol("clng", G)), [eb, C0])
                tfree[ti] = es
                s_evs.append(es)
                b4_unit(G // 4, G % 4)
            S_DONE = s_evs[-1]
            yield 3, [S_DONE], {"s": cvs[:].rearrange("p a b -> p (a b)")}
            ZC_DONE = zc_evs[-1]

            pw_banks = [next_bank() for _ in range(4)]
            b_stats = [next_bank(), next_bank()]
            y_evs = []
            units = []

            def b5_front(u, cp, h, ev_pw):
                bh = pw_banks[h]
                si = u % 4
                qi = u % 2
                yc, rr = f32t[si * 2], f32t[si * 2 + 1]
                prev = units[u - 4]["ey"] if u >= 4 else None
                eyc = S.op("act", lambda e, bh=bh, yc=yc, cp=cp: e.activation(out=yc[:], in_=bank(bh), func=AF.Identity,
                                                                                bias=vcol("pwb", cp), scale=1.0), [ev_pw, C0, prev, e6, s_evs[-1]])
                last_bank_reader[bh] = eyc
                eys = S.op("pool", lambda e, yc=yc, qi=qi: e.tensor_tensor(out=sqs[qi][:], in0=yc[:], in1=yc[:], op=ALU.mult), [eyc, sq_free[qi]])
                bs = b_stats[qi]
                est = S.op("pe", lambda e, qi=qi, bs=bs: e.matmul(bank(bs), lhsT=bd64s, rhs=sqs[qi][:], start=True, stop=True),
                           [eys, C0, bank_dep(bs)])
                sq_free[qi] = est
                units.append({"cp": cp, "h": h, "eyc": eyc, "est": est, "si": si, "qi": qi})

            def b5_back(u):
                info = units[u]
                cp, h, si, qi = info["cp"], info["h"], info["si"], info["qi"]
                yc, rr = f32t[si * 2], f32t[si * 2 + 1]
                bs = b_stats[qi]
                el = S.op("act", lambda e, rr=rr, bs=bs: e.activation(out=rr[:], in_=bank(bs), func=AF.Ln, bias=epsT, scale=1.0), [info["est"]])
                last_bank_reader[bs] = el
                er = S.op("act", lambda e, rr=rr: e.activation(out=rr[:], in_=rr[:], func=AF.Exp, scale=-0.5), [el])
                et = S.op("dve", lambda e, yc=yc, rr=rr: e.tensor_tensor(out=yc[:], in0=yc[:], in1=rr[:], op=ALU.mult), [info["eyc"], er, info["est"]])
                zsrc = bass.AP(zc, cp * TM + h, [[4 * TM, 128], [4, 512]])
                ydst = bass.AP(yT, cp * TM + h, [[8 * TM, 128], [4, 512]])
                ey = S.op("dve", lambda e, yc=yc, cp=cp, zsrc=zsrc, ydst=ydst: e.scalar_tensor_tensor(
                    out=ydst, in0=yc[:], scalar=vcol("gcg", cp), in1=zsrc, op0=ALU.mult, op1=ALU.mult),
                    [et, ZC_DONE, C0, u_evs[-1]])
                info["ey"] = ey
                y_evs.append(ey)

            e_wv = S.dma("pool", lambda e: e.dma_start(out=wv[:], in_=win_d[:, 2560:3072].rearrange("(k p) c -> p k c", p=128)), [st_last], "wv")
            v_banks = [b for b in range(8) if b not in pw_banks and b not in b_stats]
            v_evs = []

            def v_unit(T):
                b = v_banks[T % len(v_banks)]
                evp = None
                for k in range(8):
                    evp = S.op("pe", lambda e, b=b, k=k, T=T: e.matmul(bank(b), lhsT=hnT[:, k, T * 128:(T + 1) * 128], rhs=wv[:, k, :],
                                                                       start=(k == 0), stop=(k == 7)), [e_wv, bank_dep(b), st_last] + HN, sig=(k == 7))
                evv = S.op("dve", lambda e, b=b, T=T: e.tensor_tensor(out=vp[:, T, :, :].rearrange("p a b -> p (a b)"), in0=bank(b), in1=bvb[:], op=ALU.add),
                           [evp, C0, st_last])
                last_bank_reader[b] = evv
                v_evs.append(evv)

            vT = [0]

            def v_some(n):
                for _ in range(n):
                    if vT[0] < NTILE:
                        v_unit(vT[0])
                        vT[0] += 1

            u = 0
            for cp in range(4):
                evh = [None] * 4
                for G in range(16):
                    for h in range(4):
                        evh[h] = S.op("pe", lambda e, cp=cp, G=G, h=h: e.matmul(
                            bank(pw_banks[h]), lhsT=pwr[32 * h:32 * h + 32, G, cp * 128:(cp + 1) * 128],
                            rhs=cvs[32 * h:32 * h + 32, G, :], start=(G == 0), stop=(G == 15), tile_position=(32 * h, 0)),
                            [S_DONE, bank_dep(pw_banks[h])] + e_pw, sig=(G == 15))
                for h in range(4):
                    b5_front(u, cp, h, evh[h])
                    if u >= 1:
                        b5_back(u - 1)
                    u += 1
                    v_some(1 if h < 3 else 2)
            b5_back(u - 1)
            v_some(NTILE)
            V_DONE = v_evs[-1]
            er = None
            YC_DONE = y_evs[-1]
            yield 4, [YC_DONE], {"yT": yT[:].rearrange("p a b -> p (a b)"), "zc": zc[:].rearrange("p a b -> p (a b)")}

            bar_pe = S.op("pe", lambda e: e.matmul(bank(b_stats[0])[:, 0:8], lhsT=ident, rhs=ident[:, 0:8], start=True, stop=True),
                          [YC_DONE, bank_dep(b_stats[0]), bank_dep(b_stats[1])])
            bar_act = S.op("act", lambda e: e.activation(out=smalls[:, 10:11], in_=smalls[:, 9:10], func=AF.Copy), [YC_DONE, bar_pe])
            bar_dve = S.op("dve", lambda e: e.tensor_copy(out=smalls[:, 11:12], in_=smalls[:, 9:10]), [bar_act, bar_pe])
            BAR = [bar_pe, bar_act, bar_dve]
            last_bank_reader[:] = [bar_dve] * 8

            e_tt = big_dma("pool", ttx[:].rearrange("p a b -> p (a b)"), ttx_d, BAR, "tt", step=1792)
            e_rv = S.dma("pool", lambda e: e.dma_start(out=rvt[:].rearrange("p a b -> p (a b)"), in_=rv_d), BAR, "rv")
            e_wos = []
            for nh_ in range(2):
                e_wo = S.dma("pool", lambda e, nh_=nh_: e.dma_start(out=wo[:, :, nh_ * 512:(nh_ + 1) * 512], in_=wout_d[:, nh_ * 512:(nh_ + 1) * 512].rearrange("(k p) c -> p k c", p=128)), BAR, f"wo{nh_}")
                e_wos.append(e_wo)
            e_tx = None
            e_rw = S.dma("pool", lambda e: e.dma_start(out=rwt[:], in_=rw_d), BAR, "rw")
            e_rvf = None
            for cls_, qb_ in ((0, 0), (1, 7)):
                e_rvf = S.op("dve", lambda e, cls_=cls_, qb_=qb_: e.tensor_copy(
                    out=rvf[:, cls_, :].rearrange("p (a c) -> p a c", a=24), in_=rvt[:, qb_, :].unsqueeze(2).to_broadcast([128, 24, 64])),
                    BAR + [e_rv, e_rvf])
            e_z1 = S.op("pool", lambda e: e.memset(qzA[64:128, :], 0.0), BAR)
            e_z2 = S.op("pool", lambda e: e.memset(qzB[0:64, :], 0.0), BAR + [e_z1])
            e_pz = e_z2
            for i_ in range(4):
                e_pz = S.op("pool", lambda e, i_=i_: e.memset(pT[i_][:, 0:192], 0.0), BAR + [e_pz])
                e_pz = S.op("pool", lambda e, i_=i_: e.memset(pT[i_][:, 1408:1536], 0.0), BAR + [e_pz])

            SB_A, SB_B, B_OS, B_ST = 0, 3, 6, 7
            pair_done = None
            pT_free = [None, None, None, None]
            st_state = {"os_free": None, "st_free": None, "sq_free": [None, None, None]}
            tw_free = None
            sA_free = None
            sB_free = None
            os_free = None
            st_free = None
            att_tail = []
            for pp in range(int(os.environ.get('MK_NPP', 4))):
                sq_, eq_ = take_chunk()
                sk_, ek_ = take_chunk()
                sz_, ez_ = take_chunk()
                if os.environ.get("MK_C1", "") == "wonly" and pp >= 1:
                    release_chunk(sq_, None); release_chunk(sk_, None); release_chunk(sz_, None)
                    continue
                proj_banks = [0, 1, 2, 3, 4, 5]
                pbi = [0]

                def nb_():
                    b = proj_banks[pbi[0] % 6]
                    pbi[0] += 1
                    return b
                extra = [pair_done] if pair_done is not None else []
                q_evs = []
                NKQ = int(os.environ.get("MK_NKQ", 8)) if pp >= 1 else 8
                for tt in range(4 if NKQ == 8 or pp == 0 else 1):
                    b = nb_()
                    for k in range(NKQ):
                        evp = S.op("pe", lambda e, b=b, k=k, tt=tt, sq_=sq_: e.matmul(
                            bank(b), lhsT=wsl[sq_][:, k, :], rhs=hnT[:, k, HALO + tt * 512:HALO + (tt + 1) * 512],
                            start=(k == 0), stop=(k == 7)), [eq_, bank_dep(b)] + HN + BAR, sig=(k == NKQ - 1))
                    if os.environ.get("MK_C1", "") == "qafter" and pp >= 1:
                        continue
                    if "qev" in SKIP and pp >= 1:
                        last_bank_reader[b] = evp
                        q_evs.append(evp)
                        continue
                    e_qa = S.op("act", lambda e, b=b, tt=tt, pp=pp: e.activation(out=qzA[0:64, tt * 512:(tt + 1) * 512], in_=ps[0:64, b * 512:(b + 1) * 512],
                                                                                   func=AF.Identity, bias=bq8[0:64, pp:pp + 1], scale=0.125), [evp, e_bq8] + BAR + extra)
                    e_qb = S.op("act", lambda e, b=b, tt=tt, pp=pp: e.activation(out=qzB[64:128, tt * 512:(tt + 1) * 512], in_=ps[64:128, b * 512:(b + 1) * 512],
                                                                                   func=AF.Identity, bias=bq8[64:128, pp:pp + 1], scale=0.125), [evp, e_bq8, e_z2] + BAR + extra)
                    last_bank_reader[b] = e_qb
                    q_evs.append(e_qb)
                if os.environ.get("MK_C1", "") == "qafter" and pp >= 1:
                    break
                release_chunk(sq_, evp)
                k_evs = []
                for tt in range(5):
                    b = nb_()
                    for k in range(8):
                        evp = S.op("pe", lambda e, b=b, k=k, tt=tt, sk_=sk_: e.matmul(
                            bank(b), lhsT=wsl[sk_][:, k, :], rhs=hnT[:, k, tt * 512:(tt + 1) * 512],
                            start=(k == 0), stop=(k == 7)), [ek_, bank_dep(b)] + HN + BAR, sig=(k == 7))
                    if "kev" in SKIP and pp >= 1:
                        last_bank_reader[b] = evp
                        k_evs.append(evp)
                        continue
                    e_k = S.op("dve", lambda e, b=b, tt=tt, pp=pp: e.tensor_scalar(out=kT[:, tt * 512:(tt + 1) * 512], in0=bank(b), scalar1=vcol("bk", pp),
                                                                                     scalar2=None, op0=ALU.add), [evp, C0] + BAR + extra)
                    last_bank_reader[b] = e_k
                    k_evs.append(e_k)
                release_chunk(sk_, evp)
                z_evs = []
                for tt in range(4):
                    b = nb_()
                    for k in range(8):
                        evp = S.op("pe", lambda e, b=b, k=k, tt=tt, sz_=sz_: e.matmul(
                            bank(b), lhsT=wsl[sz_][:, k, :], rhs=hnT[:, k, HALO + tt * 512:HALO + (tt + 1) * 512],
                            start=(k == 0), stop=(k == 7)), [ez_, bank_dep(b)] + HN + BAR, sig=(k == 7))
                    if "zev" in SKIP and pp >= 1:
                        last_bank_reader[b] = evp
                        z_evs.append(evp)
                        continue
                    e_z = S.op("act", lambda e, b=b, tt=tt, pp=pp: e.activation(out=za[:, tt * 512:(tt + 1) * 512], in_=bank(b), func=AF.Silu,
                                                                                  bias=vcol("bza", pp), scale=1.0), [evp, C0] + BAR + extra)
                    last_bank_reader[b] = e_z
                    z_evs.append(e_z)
                release_chunk(sz_, evp)
                e_tx = S.op("act", lambda e, pp=pp: e.activation(out=ttx[:, 2 * pp:2 * pp + 2, :], in_=ttx[:, 2 * pp:2 * pp + 2, :], func=AF.Exp), e_tt + BAR)
                QK = [q_evs[-1], q_evs[-2], k_evs[-1], V_DONE, e_tx, e_rv]
                if os.environ.get("MK_C1", "") == "projafter" and pp >= 1:
                    continue

                if os.environ.get("MK_C1", "") == "proj":
                    yield 6, QK + [z_evs[-1]], {"kT": kT[:], "qzA": qzA[:], "qzB": qzB[:], "za": za[:], "ttx": ttx[:].rearrange("p a b -> p (a b)")}

                e_tw = None
                for a_ in range(2):
                    e_tw = S.op("dve", lambda e, a_=a_, pp=pp: e.tensor_tensor(out=ttw[:, a_, :], in0=ttx[:, 2 * pp + a_, :], in1=rwt[:], op=ALU.mult),
                                [e_tx, e_rw, tw_free, e_tw])
                blk = {}

                def att_S(qb):
                    interior = 1 <= qb <= 6
                    info = {"masks": [], "pi": []}
                    for hd, (SB, qz) in enumerate(((SB_A, qzA), (SB_B, qzB))):
                        for sl in range(6):
                            kt = 5 - sl
                            lo, hi = (0, 4)
                            if interior:
                                lo, hi = {0: (3, 4), 5: (0, 2)}.get(sl, (0, 4))
                            c0 = SB * 512 + sl * 256
                            evp = S.op("pe", lambda e, c0=c0, lo=lo, hi=hi, kt=kt, qb=qb, qz=qz: e.matmul(
                                ps[:, c0 + lo * 64:c0 + hi * 64],
                                lhsT=kT[:, (2 * qb + kt) * 128:(2 * qb + kt + 1) * 128], rhs=qz[:, qb * 256 + lo * 64:qb * 256 + hi * 64],
                                start=True, stop=True), QK + [bank_dep(SB), bank_dep(SB + 1), bank_dep(SB + 2)], sig=(sl == 5))
                        pi = (qb % 2) * 2 + hd
                        x0, x1 = (192, 1408) if interior else (0, 1536)
                        e_ex = S.op("act", lambda e, SB=SB, pi=pi, x0=x0, x1=x1: e.activation(
                            out=pT[pi][:, x0:x1], in_=ps[:, SB * 512 + x0:SB * 512 + x1], func=AF.Exp), [evp, pT_free[pi], e_pz])
                        for bb_ in range(3):
                            last_bank_reader[SB + bb_] = e_ex
                        info["last_ex"] = e_ex
                        h = 2 * pp + hd
                        pview = pT[pi][:].rearrange("p (a b c) -> p a b c", a=6, b=4)
                        if interior:
                            tview = bass.AP(ttw, hd * 896, [[2 * 896, 128], [128, 6], [64, 4], [1, 64]])
                            e_mk = S.op("dve", lambda e, pview=pview, tview=tview: e.tensor_tensor(out=pview, in0=pview, in1=tview, op=ALU.mult), [e_ex, e_tw])
                        else:
                            tview = bass.AP(ttx, h * 896, [[8 * 896, 128], [128, 6], [64, 4], [1, 64]])
                            e_m1_ = S.op("dve", lambda e, pview=pview, tview=tview: e.tensor_tensor(out=pview, in0=pview, in1=tview, op=ALU.mult), [e_ex, e_tx])
                            cls_ = 0 if qb == 0 else 1
                            e_mk = S.op("dve", lambda e, pi=pi, cls_=cls_: e.tensor_tensor(out=pT[pi][:], in0=pT[pi][:], in1=rvf[:, cls_, :], op=ALU.mult), [e_m1_, e_rvf])
                        info["masks"].append(e_mk)
                        info["pi"].append(pi)
                    blk[qb] = info

                def att_PV(qb):
                    nonlocal_os = st_state
                    interior = 1 <= qb <= 6
                    info = blk[qb]
                    (piA, piB) = info["pi"]
                    deps = info["masks"] + [V_DONE, st_state["os_free"], e_sm]
                    e_pv = None
                    o0 = B_OS * 512

                    def rng(sl):
                        return (0, 4)
                    for sl in range(6):
                        Tk = 2 * qb + 5 - sl
                        lo, hi = rng(sl)
                        first = (sl == 0)
                        lastk = (sl == 5)
                        S.op("pe", lambda e, sl=sl, Tk=Tk, lo=lo, hi=hi, first=first, lastk=lastk, pp=pp, piA=piA: e.matmul(
                            ps[0:64, o0 + lo * 64:o0 + hi * 64], lhsT=vp[:, Tk, pp, 0:64], rhs=pT[piA][:, sl * 256 + lo * 64:sl * 256 + hi * 64],
                            start=first, stop=lastk, tile_position=(0, 0)), deps, sig=False)
                        S.op("pe", lambda e, sl=sl, Tk=Tk, lo=lo, hi=hi, first=first, lastk=lastk, pp=pp, piB=piB: e.matmul(
                            ps[64:128, o0 + lo * 64:o0 + hi * 64], lhsT=vp[:, Tk, pp, 64:128], rhs=pT[piB][:, sl * 256 + lo * 64:sl * 256 + hi * 64],
                            start=first, stop=lastk, tile_position=(0, 64)), deps, sig=False)
                    for sl in range(6):
                        lo, hi = rng(sl)
                        lastk = (sl == 5)
                        S.op("pe", lambda e, sl=sl, lo=lo, hi=hi, lastk=lastk, piA=piA: e.matmul(
                            ps[0:64, o0 + 256 + lo * 64:o0 + 256 + hi * 64], lhsT=ones64[:], rhs=pT[piA][:, sl * 256 + lo * 64:sl * 256 + hi * 64],
                            start=(sl == 0), stop=lastk, tile_position=(0, 0)), deps, sig=False)
                        e_pv = S.op("pe", lambda e, sl=sl, lo=lo, hi=hi, lastk=lastk, piB=piB: e.matmul(
                            ps[64:128, o0 + 256 + lo * 64:o0 + 256 + hi * 64], lhsT=ones64[:], rhs=pT[piB][:, sl * 256 + lo * 64:sl * 256 + hi * 64],
                            start=(sl == 0), stop=lastk, tile_position=(0, 64)), deps, sig=lastk)
                    pT_free[piA] = e_pv
                    pT_free[piB] = e_pv
                    oi = qb % 3
                    e_s2 = S.op("act", lambda e, oi=oi: e.activation(out=ess[oi][:], in_=ps[:, B_OS * 512 + 256:B_OS * 512 + 512], func=AF.Square, scale=1e-3),
                                [e_pv, st_state["sq_free"][oi]])
                    e_oc = S.op("dve", lambda e, qb=qb: e.tensor_copy(out=oS[:, qb * 256:(qb + 1) * 256], in_=ps[:, B_OS * 512:B_OS * 512 + 256]),
                                [e_pv, pair_done, e_s2])
                    st_state["os_free"] = e_oc
                    e_o2 = S.op("dve", lambda e, qb=qb, oi=oi: e.tensor_tensor(out=osq[oi][:], in0=oS[:, qb * 256:(qb + 1) * 256], in1=oS[:, qb * 256:(qb + 1) * 256], op=ALU.mult),
                                [e_oc, st_state["sq_free"][oi]])
                    info["e_s2"] = e_s2
                    info["e_o2"] = e_o2
                    info["e_oc"] = e_oc

                def att_stat(qb):
                    info = blk[qb]
                    oi = qb % 3
                    S.op("pe", lambda e, oi=oi: e.matmul(ps[:, B_ST * 512:B_ST * 512 + 256], lhsT=bd64s, rhs=osq[oi][:], start=True, stop=False),
                         [info["e_o2"], C0, st_state["st_free"]], sig=False)
                    e_stm = S.op("pe", lambda e, oi=oi: e.matmul(ps[:, B_ST * 512:B_ST * 512 + 256], lhsT=ident, rhs=ess[oi][:], start=False, stop=True),
                                 [info["e_s2"], C0], sig=True)
                    st_state["sq_free"][oi] = e_stm
                    info["e_stm"] = e_stm

                def att_stcopy(qb):
                    info = blk[qb]
                    ev = S.op("act", lambda e, qb=qb: e.activation(out=stS[:, qb * 256:(qb + 1) * 256], in_=ps[:, B_ST * 512:B_ST * 512 + 256], func=AF.Copy),
                              [info["e_stm"], pair_done])
                    st_state["st_free"] = ev
                    return ev

                NQ = 8
                att_S(0)
                last_sc = None
                for n in range(NQ):
                    if n >= 2:
                        last_sc = att_stcopy(n - 2)
                    if n + 1 < NQ:
                        att_S(n + 1)
                    att_PV(n)
                    if n >= 1:
                        att_stat(n - 1)
                if NQ >= 2:
                    last_sc = att_stcopy(NQ - 2)
                att_stat(NQ - 1)
                last_sc = att_stcopy(NQ - 1)
                e_ex = blk[NQ - 1]["last_ex"]
                e_l = S.op("act", lambda e: e.activation(out=stS[:], in_=stS[:], func=AF.Ln), [last_sc])
                e_r = S.op("act", lambda e: e.activation(out=stS[:], in_=stS[:], func=AF.Exp, scale=-0.5), [e_l])
                e_t = S.op("dve", lambda e: e.tensor_tensor(out=oS[:], in0=oS[:], in1=stS[:], op=ALU.mult), [e_r, blk[NQ - 1]["e_oc"], blk[NQ - 1]["e_o2"]])
                e_y = S.op("dve", lambda e, pp=pp: e.scalar_tensor_tensor(out=yT[:, 4 + pp, :], in0=oS[:], scalar=vcol("gag", pp), in1=za[:],
                                                                           op0=ALU.mult, op1=ALU.mult), [e_t, z_evs[-1], C0])
                for _d in range(int(os.environ.get("MK_PAD", 0))):
                    e_y = S.op("dve", lambda e: e.tensor_copy(out=smalls[:, 12:13], in_=smalls[:, 9:10]), [e_y])
                pair_done = e_y
                tw_free = e_y
                for b in range(6):
                    last_bank_reader[b] = e_ex
            ATT_DONE = pair_done
            yield 6, [ATT_DONE], {"yT": yT[:].rearrange("p a b -> p (a b)"), "oS": oS[:], "stS": stS[:], "za": za[:]}

            e_fg = S.dma("sp", lambda e: e.dma_start(out=lgfg[:], in_=fg_d), [ATT_DONE, e_xn], "fg")
            barD_pe = S.op("pe", lambda e: e.matmul(bank(7)[:, 0:8], lhsT=ident, rhs=ident[:, 0:8], start=True, stop=True), [ATT_DONE, st_free, os_free, e_ex])
            barD_act = S.op("act", lambda e: e.activation(out=smalls[:, 10:11], in_=smalls[:, 9:10], func=AF.Copy), [ATT_DONE, barD_pe])
            barD = [ATT_DONE, barD_pe, barD_act]
            junkDP = ps[:, 6 * 512:8 * 512]
            xr_free = [None, None, None]
            ht_free = [None, None, None]
            ot_free = [None, None]
            dbanks = [(0, 1), (2, 3), (4, 5)]
            d_free = [barD_act] * 3
            out_evs = []
            e_h = [None] * 16
            e_ssq = [None] * 16
            e_pw2 = [None] * 16

            def d_main(T):
                i2 = T % 2
                i3 = T % 3
                e_xr = S.dma("sp", lambda e, T=T, i3=i3: e.dma_start(out=xrs[i3][:], in_=x_d[HALO + T * 128:HALO + (T + 1) * 128, :]),
                             barD + [xr_free[i3]], f"xr{i3}")
                bi = T % 3
                b0 = dbanks[bi][0]
                evp = None
                for nh in range(2):
                    for k in range(8):
                        evp = S.op("pe", lambda e, b0=b0, nh=nh, k=k, T=T: e.matmul(
                            bank(b0 + nh), lhsT=yT[:, k, T * 128:(T + 1) * 128], rhs=wo[:, k, nh * 512:(nh + 1) * 512],
                            start=(k == 0), stop=(k == 7)), barD + e_wos + [d_free[bi], YC_DONE], sig=(k == 7 and nh == 1))
                e_h[T] = S.op("dve", lambda e, b0=b0, i2=i2, i3=i3: e.tensor_tensor(out=hts[i3][:], in0=ps[:, b0 * 512:b0 * 512 + 1024], in1=xrs[i3][:], op=ALU.add),
                              [evp, e_xr, ht_free[i3]])
                d_free[bi] = e_h[T]
                xr_free[i3] = e_h[T]
                e_ssq[T] = S.op("act", lambda e, T=T, i3=i3: e.activation(out=junkDP, in_=hts[i3][:], func=AF.Square, accum_out=ssD[:, T:T + 1]), [e_h[T], barD_act, e_ssq[T - 1] if T >= 1 else None])

            def d_rs(T):
                e_r1 = S.op("dve", lambda e, T=T: e.tensor_scalar(out=rsD[:, T:T + 1], in0=ssD[:, T:T + 1], scalar1=1.0 / D, scalar2=EPS,
                                                                   op0=ALU.mult, op1=ALU.add), [e_ssq[T]])
                e_pw2[T] = S.op("pool", lambda e, T=T: e.tensor_tensor(out=rsD[:, T:T + 1], in0=rsD[:, T:T + 1], in1=mhalf, op=ALU.pow), [e_r1])

            def d_out(T):
                i2 = T % 2
                i3 = T % 3
                e_o = S.op("dve", lambda e, T=T, i2=i2, i3=i3: e.scalar_tensor_tensor(out=ots[i2][:], in0=hts[i3][:], scalar=rsD[:, T:T + 1], in1=lgfg[:],
                                                                                       op0=ALU.mult, op1=ALU.mult), [e_pw2[T], e_fg, ot_free[i2]])
                ht_free[i3] = e_o
                e_st = S.dma("act", lambda e, T=T, i2=i2: e.dma_start(out=out_d[T * 128:(T + 1) * 128, :], in_=ots[i2][:]), [e_o], f"o{i2}")
                ot_free[i2] = e_st
                out_evs.append(e_st)

            for it in range(16 + 2):
                if it < 16:
                    d_main(it)
                if 0 <= it - 1 < 16:
                    d_rs(it - 1)
                if 0 <= it - 2 < 16:
                    d_out(it - 2)
            fin = out_evs[-2:]
            yield 99, fin, {}

        final = None
        for stg, fdeps, dumps in prog():
            if stg >= stage:
                final = (fdeps, dumps)
                break
        fdeps, dumps = final
        fin = list(fdeps)
        for nm, src in dumps.items():
            dd_ = nc.dram_tensor('dbg_' + nm, [128, int(np.prod(src.shape[1:]))], src.dtype, kind='ExternalOutput').ap()
            n_ = dd_.shape[1]
            for c0 in range(0, n_, 2048):
                c1 = min(n_, c0 + 2048)
                ev_ = S.dma('sp', lambda e, dd_=dd_, src=src, c0=c0, c1=c1: e.dma_start(out=dd_[:, c0:c1], in_=src[:, c0:c1]), list(fdeps), 'dbg_' + nm)
            fin.append(ev_)
        S.wait("sp", fin)
        S.emit(nc, st)
    return nc


def _host_consts():
    ident = np.eye(128, dtype=np.float32)
    p = np.arange(128)
    bd32 = (p[:, None] // 32 == p[None, :] // 32).astype(np.float32)
    bd64s = (p[:, None] // 64 == p[None, :] // 64).astype(np.float32) / 64.0
    return np.concatenate([ident, bd32, bd64s], axis=1).astype(ml_dtypes.bfloat16)


def _ph(vec512):
    a = np.asarray(vec512, np.float32).reshape(16, 32)
    return np.tile(a.T, (4, 1))


def _cm(vec, n):
    return np.ascontiguousarray(np.asarray(vec, np.float32).reshape(n, 128).T)


def _shared_layouts(inp):
    w_in = np.ascontiguousarray(inp["w_in"][0], dtype=np.float32)
    b_in = np.asarray(inp["b_in"][0], np.float32)
    vecs = np.zeros((128, NVEC), np.float32)

    def put(name, arr):
        o, w = VC[name]
        assert arr.shape == (128, w), (name, arr.shape)
        vecs[:, o:o + w] = arr
    put("bua", _ph(b_in[0:512]))
    put("bub", _ph(b_in[512:1024]))
    put("bzc", _cm(b_in[1024:1536], 4))
    put("bq", _cm(b_in[1536:2048], 4))
    put("bk", _cm(b_in[2048:2560], 4))
    put("bza", _cm(b_in[3072:3584], 4))
    put("dwb", _ph(inp["dw_b"][0]))
    put("clng", _ph(inp["cln_g"][0]))
    put("clnb", _ph(inp["cln_b"][0]))
    put("pwb", _cm(inp["pw_b"][0], 4))
    put("gcg", _cm(inp["gn_conv_g"][0], 4))
    put("gag", _cm(inp["gn_att_g"][0], 4))
    bv_b = np.ascontiguousarray(np.broadcast_to(b_in[2560:3072][None, :], (128, 512)), dtype=np.float32)
    lng_b = np.ascontiguousarray(np.broadcast_to(np.asarray(inp["ln_g"][0], np.float32)[None, :], (128, D)))
    fg_b = np.ascontiguousarray(np.broadcast_to(np.asarray(inp["final_g"], np.float32)[None, :], (128, D)))
    pw = np.asarray(inp["pw_w"][0], np.float32).reshape(16, 32, 512)
    pw_rep = np.tile(pw.transpose(1, 0, 2), (4, 1, 1)).reshape(128, 16 * 512)
    dw = np.asarray(inp["dw_w"][0], np.float32)
    wc = np.zeros((4, 32, 16, 9, 4, 32), np.float32)
    cidx = np.arange(32)
    for g in range(4):
        for h in range(4):
            for di in range(9):
                j = 4 * (di - 4) + g - h + 15
                if 0 <= j <= 30:
                    for G in range(16):
                        wc[g, cidx, G, di, h, cidx] = dw[j, G * 32:(G + 1) * 32]
    wconv = np.ascontiguousarray(wc.reshape(128, 144 * 128))
    rpb = np.asarray(inp["rpb"][0], np.float32)
    c = np.arange(64)
    w = np.arange(64)
    cs = np.clip(w - 8, 0, 48)
    colok = (c[:, None] >= cs[None, :]) & (c[:, None] < cs[None, :] + 16)
    crel = np.clip(c[:, None] - w[None, :] + 15, 0, 30)
    ttxb = np.full((2, 64, 8, 14, 64), NEG, np.float32)
    for i2 in range(2):
        for e in range(14):
            delta = 6 - e + i2
            for h in range(8):
                vals = rpb[h, delta + 7][crel]
                ttxb[i2, :, h, e, :] = np.where(colok, vals, NEG)
    ttxb = np.ascontiguousarray(ttxb.reshape(128, 8 * 896))
    rwin = np.zeros((2, 64, 14), np.float32)
    for i2 in range(2):
        for e in range(14):
            if -4 <= 6 - e + i2 <= 3:
                rwin[i2, :, e] = 1.0
    rwin = np.ascontiguousarray(np.repeat(rwin.reshape(128, 14, 1), 64, axis=2).reshape(128, 896))
    return dict(w_in=w_in, w_out=np.ascontiguousarray(inp["w_out"][0], dtype=np.float32), pw_rep=np.ascontiguousarray(pw_rep),
                wconv=wconv, constb=_host_consts(), vecs=vecs, bv_b=bv_b, lng_b=lng_b, fg_b=fg_b, ttxb=ttxb, rwin=rwin)


def _core_layouts(inp, core):
    b, j = core // 4, core % 4
    x = np.asarray(inp["x"], np.float32)
    lo = j * TM - HALO
    x_ext = np.zeros((NT, D), np.float32)
    s0, s1 = max(lo, 0), min(lo + NT, SEQ)
    x_ext[s0 - lo:s1 - lo] = x[b, s0:s1]
    R0 = 32 * j
    rv = np.zeros((2, 64, 8, 6, 4), np.float32)
    for qb in range(8):
        for sl in range(6):
            kt = 5 - sl
            for r4 in range(4):
                r = R0 + 4 * qb + r4
                rs = min(max(r - 4, 0), 120)
                for i2 in range(2):
                    kr = R0 - 4 + 4 * qb + 2 * kt + i2
                    if rs <= kr < rs + 8:
                        rv[i2, :, qb, sl, r4] = 1.0
    rv = np.ascontiguousarray(rv.reshape(128, 8 * 24))
    uvalid = np.ones((128, 8), np.float32)
    if j == 0:
        uvalid[:, 0:4] = 0.0
    if j == 3:
        uvalid[:, 4:8] = 0.0
    return dict(x_ext=x_ext, rv=rv, uvalid=uvalid)


_NC_CACHE = {}


def kernel(**inputs):
    shared = _shared_layouts(inputs)
    in_maps = []
    for core in range(NCORE):
        m = dict(shared)
        m.update(_core_layouts(inputs, core))
        in_maps.append(m)
    if DEBUG not in _NC_CACHE:
        _NC_CACHE[DEBUG] = build_nc(debug=DEBUG)
    nc = _NC_CACHE[DEBUG]
    res = run_bass_kernel_spmd(nc, in_maps, core_ids=list(range(NCORE)))
    out = np.empty((2, SEQ, D), np.float32)
    for core in range(NCORE):
        b, j = core // 4, core % 4
        out[b, j * TM:(j + 1) * TM] = res.results[core]["out"]
    if DEBUG:
        kernel.last_results = res.results
    return out
```

```python
import os
from contextlib import ExitStack

import numpy as np
import ml_dtypes

import concourse.bass as bass
import concourse.mybir as mybir
from concourse.bass_utils import run_bass_kernel_spmd

F32 = mybir.dt.float32
BF16 = mybir.dt.bfloat16
AF = mybir.ActivationFunctionType
ALU = mybir.AluOpType

D = 1024
SEQ = 8192
NCORE = 8
TM = 2048
HALO = 256
NT = TM + 2 * HALO
NTILE = NT // 128
PROJ = 3584
EPS = 1e-6
UW = 520
NEG = -30000.0

ENGS = ["pe", "act", "dve", "pool", "sp"]


class Sched:
    def __init__(self):
        self.ops = {e: [] for e in ENGS}
        self.cnt = {e: 0 for e in ENGS}
        self.dcnt = {}
        self.pool_fifo = []
        self.max_swdge = 6

    def op(self, eng, fn, deps=(), sig=True):
        ev = None
        if sig:
            self.cnt[eng] += 1
            ev = (eng, self.cnt[eng])
        self.ops[eng].append((fn, [d for d in deps if d is not None], sig, None))
        return ev

    def dma(self, eng, fn, deps, sem):
        self.dcnt[sem] = self.dcnt.get(sem, 0) + 16
        ev = ("d:" + sem, self.dcnt[sem])
        deps = [d for d in deps if d is not None]
        if eng == "pool":
            if len(self.pool_fifo) >= self.max_swdge:
                deps.append(self.pool_fifo.pop(0))
            self.pool_fifo.append(ev)
        self.ops[eng].append((fn, deps, False, "d:" + sem))
        return ev

    def wait(self, eng, deps):
        self.ops[eng].append((None, [d for d in deps if d is not None], False, None))

    def emit(self, nc, stack):
        names = list(ENGS) + ["d:" + s for s in self.dcnt]
        sems = {}
        for n in names:
            sems[n] = stack.enter_context(nc.semaphore("s_" + n.replace(":", "_")))
        block = stack.enter_context(nc.Block())
        ops = self.ops

        def run(engname, engine):
            seen = {}
            for fn, deps, sig, dsem in ops[engname]:
                need = {}
                for (src, v) in deps:
                    if v > seen.get(src, 0):
                        need[src] = max(need.get(src, 0), v)
                for src, v in need.items():
                    engine.wait_ge(sems[src], v)
                    seen[src] = v
                if fn is None:
                    continue
                ins = fn(engine)
                if sig:
                    ins.then_inc(sems[engname], 1)
                elif dsem is not None:
                    ins.then_inc(sems[dsem], 16)

        @block.tensor
        def _(e):
            run("pe", e)

        @block.scalar
        def _(e):
            run("act", e)

        @block.vector
        def _(e):
            run("dve", e)

        @block.gpsimd
        def _(e):
            run("pool", e)

        @block.sync
        def _(e):
            run("sp", e)


VC = {}
_o = 0
for _n, _w in [("bua", 16), ("bub", 16), ("bzc", 4), ("bq", 4), ("bk", 4), ("bza", 4),
               ("dwb", 16), ("clng", 16), ("clnb", 16), ("pwb", 4), ("gcg", 4), ("gag", 4)]:
    VC[_n] = (_o, _w)
    _o += _w
NVEC = _o

DEBUG = bool(int(os.environ.get("MK_DEBUG", "0")))


def build_nc(debug=False, stage=99):
    nc = bass.Bass("TRN2", target_bir_lowering=False)
    dt = nc.dram_tensor
    x_d = dt("x_ext", [NT, D], F32, kind="ExternalInput").ap()
    win_d = dt("w_in", [D, PROJ], F32, kind="ExternalInput").ap()
    wout_d = dt("w_out", [D, D], F32, kind="ExternalInput").ap()
    pwr_d = dt("pw_rep", [128, 16 * 512], F32, kind="ExternalInput").ap()
    wc_d = dt("wconv", [128, 144 * 128], F32, kind="ExternalInput").ap()
    cb_d = dt("constb", [128, 3 * 128], BF16, kind="ExternalInput").ap()
    vec_d = dt("vecs", [128, NVEC], F32, kind="ExternalInput").ap()
    bv_d = dt("bv_b", [128, 512], F32, kind="ExternalInput").ap()
    lng_d = dt("lng_b", [128, D], F32, kind="ExternalInput").ap()
    fg_d = dt("fg_b", [128, D], F32, kind="ExternalInput").ap()
    ttx_d = dt("ttxb", [128, 8 * 896], F32, kind="ExternalInput").ap()
    rv_d = dt("rv", [128, 8 * 24], F32, kind="ExternalInput").ap()
    uv_d = dt("uvalid", [128, 8], F32, kind="ExternalInput").ap()
    rw_d = dt("rwin", [128, 896], F32, kind="ExternalInput").ap()
    out_d = dt("out", [TM, D], F32, kind="ExternalOutput").ap()
    S = Sched()
    with ExitStack() as st:
        cur = [(nc.sbuf_base + 63) // 64 * 64]
        top = nc.sbuf_top
        OFF = {}

        def alloc(name, shape, dtype, at=None):
            nbytes = int(np.prod(shape[1:])) * (2 if dtype == BF16 else 4)
            nbytes = (nbytes + 63) // 64 * 64
            if at is None:
                off = cur[0]
                cur[0] += nbytes
            else:
                off = at
            assert off + nbytes <= top, (name, off, nbytes, top)
            t = nc.alloc_sbuf_tensor_at(name, list(shape), dtype, offset=off)
            OFF[name] = off
            return t

        hnT = alloc("hnT", [128, 8, NT], BF16)
        regY = cur[0]
        hnTp = alloc("hnTp", [128, 8, 4, UW], BF16)
        yT = alloc("yT", [128, 8, TM], BF16, at=regY)
        cst = alloc("cst", [128, 3, 128], BF16)
        ones64 = alloc("ones64", [128, 64], BF16)
        vecs = alloc("vecs", [128, NVEC], F32)
        bq8 = alloc("bq8", [128, 4], F32)
        bvb = alloc("bvb", [128, 512], F32)
        lgfg = alloc("lgfg", [128, D], F32)
        smalls = alloc("smalls", [128, 16], F32)
        uvt = alloc("uvt", [128, 8], F32)
        ssA = alloc("ssA", [128, NTILE], F32)
        rsA = alloc("rsA", [128, NTILE], F32)
        ssD = alloc("ssD", [128, 16], F32)
        rsD = alloc("rsD", [128, 16], F32)
        NWS = 4
        wsl = [alloc(f"wsl{i}", [128, 8, 128], BF16) for i in range(NWS)]
        regS = cur[0]
        u_ph = alloc("u_ph", [128, 16, UW], BF16)
        zc = alloc("zc", [128, 4, TM], BF16, at=OFF["u_ph"])
        wcv = alloc("wcv", [128, 144, 128], BF16)
        cvs = alloc("cvs", [128, 16, 512], BF16)
        sqs = [alloc(f"sqs{i}", [128, 512], BF16) for i in range(2)]
        pwr = alloc("pwr", [128, 16, 512], BF16)
        sgm = [alloc(f"sgm{i}", [128, 260], F32) for i in range(2)]
        f32t = [alloc(f"f32t{i}", [128, 512], F32) for i in range(8)]
        conv_end = cur[0]
        cur[0] = regS
        ttx = alloc("ttx", [128, 8, 896], BF16)
        rvt = alloc("rvt", [128, 8, 24], BF16)
        rwt = alloc("rwt", [128, 896], BF16)
        assert cur[0] <= OFF["wcv"]
        cur[0] = OFF["wcv"]
        vp = alloc("vp", [128, NTILE, 4, 128], BF16)
        wv = alloc("wv", [128, 8, 512], BF16)
        kT = alloc("kT", [128, NT], BF16)
        osq = [alloc(f"osq{i}", [128, 256], BF16) for i in range(3)]
        ess = [alloc(f"ess{i}", [128, 256], BF16) for i in range(3)]
        assert cur[0] <= OFF["cvs"], (cur[0], OFF["cvs"])
        cur[0] = OFF["cvs"]
        qzA = alloc("qzA", [128, TM], BF16)
        qzB = alloc("qzB", [128, TM], BF16)
        za = alloc("za", [128, TM], BF16)
        pT = [alloc(f"pT{i}", [128, 1536], BF16) for i in range(4)]
        ttw = alloc("ttw", [128, 2, 896], BF16)
        rvf = alloc("rvf", [128, 2, 1536], BF16)
        oS = alloc("oS", [128, TM], F32)
        stS = alloc("stS", [128, TM], F32)
        wo = alloc("wo", [128, 8, D], BF16)
        att_end = cur[0]
        pa0 = OFF["cvs"]
        NXS = 8
        xts = [alloc(f"xt{i}", [128, D], F32, at=OFF["wcv"] + i * 4096) for i in range(NXS)]
        xnb = [alloc(f"xnb{i}", [128, D], BF16, at=pa0 + i * 2048) for i in range(3)]
        junkA = alloc("junkA", [128, D], BF16, at=OFF["f32t0"])
        xrs = [alloc(f"xr{i}", [128, D], F32, at=OFF["vp"] + i * 4096) for i in range(3)]
        hts = [alloc(f"ht{i}", [128, D], F32, at=OFF["vp"] + 12288 + i * 4096) for i in range(3)]
        ots = [alloc(f"ot{i}", [128, D], F32, at=OFF["vp"] + 24576 + i * 4096) for i in range(2)]
        assert OFF["vp"] + 32768 <= OFF["osq0"]
        if os.environ.get("MK_MAP"):
            print("SBUF map: regS", regS, "conv_end", conv_end, "att_end", att_end, "top", top, "free", top - max(conv_end, att_end))
        assert max(conv_end, att_end) <= top, (conv_end, att_end, top)

        ps = st.enter_context(nc.psum_tensor("ps", [128, 4096], F32))

        def bank(b, w=512):
            return ps[:, b * 512:b * 512 + w]

        ident = cst[:, 0, :]
        bd32 = cst[:, 1, :]
        bd64s = cst[:, 2, :]

        def vcol(name, j=0, n=1):
            o, w = VC[name]
            return vecs[:, o + j:o + j + n]

        mhalf = smalls[:, 0:1]
        epsT = smalls[:, 1:2]

        def prog():
            last_bank_reader = [None] * 8

            def bank_dep(b):
                return last_bank_reader[b]

            e_cst = S.dma("sp", lambda e: e.dma_start(out=cst[:].rearrange("p a b -> p (a b)"), in_=cb_d), [], "c0")
            e_vec = S.dma("sp", lambda e: e.dma_start(out=vecs[:], in_=vec_d), [], "c0")
            e_lng = S.dma("sp", lambda e: e.dma_start(out=lgfg[:], in_=lng_d), [], "c0")
            e_bv = S.dma("sp", lambda e: e.dma_start(out=bvb[:], in_=bv_d), [], "c0")
            e_uv = S.dma("sp", lambda e: e.dma_start(out=uvt[:], in_=uv_d), [], "c0")
            C0 = e_uv
            e_m0 = S.op("pool", lambda e: e.memset(smalls[:], 0.0), [])
            e_m1 = S.op("pool", lambda e: e.memset(smalls[:, 0:1], -0.5), [e_m0])
            e_m2 = S.op("pool", lambda e: e.memset(smalls[:, 1:2], EPS), [e_m1])
            e_m3 = S.op("pool", lambda e: e.memset(ones64[:], 1.0), [e_m2])
            e_m4 = S.op("pool", lambda e: e.memset(ssA[:], 0.0), [e_m3])
            e_m5 = S.op("pool", lambda e: e.memset(ssD[:], 0.0), [e_m4])
            e_sm = e_m5
            e_bq8 = S.op("dve", lambda e: e.tensor_scalar(out=bq8[:], in0=vcol("bq", 0, 4), scalar1=0.125, scalar2=None, op0=ALU.mult), [C0])

            wslot_free = [None] * NWS
            wcount = [0]

            def load_wchunk(col0):
                i = wcount[0] % NWS
                wcount[0] += 1
                ev = S.dma("pool", lambda e, i=i, col0=col0: e.dma_start(
                    out=wsl[i][:], in_=win_d[:, col0:col0 + 128].rearrange("(k p) c -> p k c", p=128)),
                    [wslot_free[i]], f"w{i}")
                return i, ev

            chunk_cols = []
            for hh_ in range(2):
                for cc in range(4):
                    chunk_cols += [512 + cc * 128, cc * 128]
            chunk_cols += [1024 + cc * 128 for cc in range(4)]
            for pp in range(4):
                chunk_cols += [1536 + pp * 128, 2048 + pp * 128, 3072 + pp * 128]
            pending = []
            nxt = [0]

            in_use = [0]

            def prefetch():
                while len(pending) + in_use[0] < NWS and nxt[0] < len(chunk_cols):
                    pending.append(load_wchunk(chunk_cols[nxt[0]]))
                    nxt[0] += 1

            def take_chunk():
                if not pending:
                    prefetch()
                slot, ev = pending.pop(0)
                in_use[0] += 1
                return slot, ev

            def release_chunk(slot, pe_ev):
                wslot_free[slot] = pe_ev
                in_use[0] -= 1
                prefetch()

            SKIP = os.environ.get("MK_SKIP", "").split(",")
            if "wpre" not in SKIP:
                prefetch()
            def big_dma(q, dst2d, src2d, deps, sem, step=2048):
                n = dst2d.shape[1]
                evs = []
                for i_, c0 in enumerate(range(0, n, step)):
                    c1 = min(n, c0 + step)
                    evs.append(S.dma(q, lambda e, c0=c0, c1=c1: e.dma_start(out=dst2d[:, c0:c1], in_=src2d[:, c0:c1]), deps, f"{sem}{i_}"))
                return evs

            rot = [[2, 3, 4, 5, 6, 7]]
            rpos = [0]

            def next_bank():
                b = rot[0][rpos[0] % len(rot[0])]
                rpos[0] += 1
                return b

            e_e2 = {}
            u_evs = []
            b1_state = {"mm": 0, "ev": 0, "chunks": None, "info": {}}

            def b1_mm():
                n = b1_state["mm"]
                hh, rem = divmod(n, 16)
                cc, gi = divmod(rem, 4)
                G = cc * 4 + gi
                n0 = hh * 260
                hdep = [e_e2[9]] if hh == 0 else [e_e2[18]]
                if gi == 0:
                    sb_, eb_ = take_chunk()
                    sa_, ea_ = take_chunk()
                    b1_state["chunks"] = (sb_, eb_, sa_, ea_)
                sb_, eb_, sa_, ea_ = b1_state["chunks"]
                bb = next_bank()
                ba = next_bank()
                evs = {}
                for (bk_, sl_, ew_) in ((bb, sb_, eb_), (ba, sa_, ea_)):
                    evp = None
                    for k in range(8):
                        for g in range(4):
                            last = (k == 7 and g == 3)
                            evp = S.op("pe", lambda e, bk_=bk_, sl_=sl_, k=k, g=g, gi=gi, n0=n0: e.matmul(
                                ps[32 * g:32 * g + 32, bk_ * 512:bk_ * 512 + 260],
                                lhsT=wsl[sl_][:, k, gi * 32:gi * 32 + 32],
                                rhs=hnTp[:, k, g, n0:n0 + 260],
                                start=(k == 0), stop=(k == 7), tile_position=(0, 32 * g)),
                                [ew_, bank_dep(bk_)] + hdep, sig=last)
                    evs[bk_] = evp
                if gi == 3:
                    release_chunk(sb_, evs[ba])
                    release_chunk(sa_, evs[ba])
                b1_state["info"][n] = (G, n0, bb, ba, evs[bb], evs[ba])
                b1_state["mm"] = n + 1

            def b1_ev():
                n = b1_state["ev"]
                G, n0, bb, ba, ev_b, ev_a = b1_state["info"].pop(n)
                si = n % 2
                e_sg = S.op("act", lambda e, bb=bb, si=si, G=G: e.activation(
                    out=sgm[si][:], in_=ps[:, bb * 512:bb * 512 + 260], func=AF.Sigmoid, bias=vcol("bub", G), scale=1.0),
                    [ev_b, C0] + ([u_evs[-2]] if len(u_evs) >= 2 else []))
                last_bank_reader[bb] = e_sg
                e_u = S.op("dve", lambda e, ba=ba, si=si, G=G, n0=n0: e.scalar_tensor_tensor(
                    out=u_ph[:, G, n0:n0 + 260], in0=ps[:, ba * 512:ba * 512 + 260], scalar=vcol("bua", G),
                    in1=sgm[si][:], op0=ALU.add, op1=ALU.mult), [ev_a, e_sg, C0])
                last_bank_reader[ba] = e_u
                u_evs.append(e_u)
                b1_state["ev"] = n + 1

            def b1_step(limit, skew):
                if b1_state["mm"] < limit:
                    while b1_state["mm"] - b1_state["ev"] > skew:
                        b1_ev()
                    b1_mm()
                    if b1_state["mm"] - b1_state["ev"] > skew:
                        b1_ev()
                    return True
                return False

            NA = NTILE
            junkP = junkA[:]
            x_free = [None] * NXS
            xn_free = [None] * 3
            e_xl = [None] * NA
            e_pow = [None] * NA
            e_xnT = [None] * NA
            e_tp = [None] * NA
            e_e1 = [None] * NA
            evac_evs = []
            tp_bank = [0, 1]

            sq_prev = [None]

            def a_load(T):
                xs = T % NXS
                e_xl[T] = S.dma("sp", lambda e, T=T, xs=xs: e.dma_start(out=xts[xs][:], in_=x_d[T * 128:(T + 1) * 128, :]),
                                [x_free[xs]], f"x{xs}")

            def a_stat(T):
                xs = T % NXS
                e_sq = S.op("act", lambda e, T=T, xs=xs: e.activation(out=junkP, in_=xts[xs][:], func=AF.Square,
                                                                        accum_out=ssA[:, T:T + 1]), [e_xl[T], e_sm, sq_prev[0]])
                sq_prev[0] = e_sq
                e_r1 = S.op("dve", lambda e, T=T: e.tensor_scalar(out=rsA[:, T:T + 1], in0=ssA[:, T:T + 1], scalar1=1.0 / D,
                                                                   scalar2=EPS, op0=ALU.mult, op1=ALU.add), [e_sq])
                e_pow[T] = S.op("pool", lambda e, T=T: e.tensor_tensor(out=rsA[:, T:T + 1], in0=rsA[:, T:T + 1], in1=mhalf,
                                                                        op=ALU.pow), [e_r1, e_sm])

            def a_norm(T):
                xs = T % NXS
                nb = T % 3
                e_xnT[T] = S.op("dve", lambda e, T=T, xs=xs, nb=nb: e.scalar_tensor_tensor(
                    out=xnb[nb][:], in0=xts[xs][:], scalar=rsA[:, T:T + 1], in1=lgfg[:], op0=ALU.mult, op1=ALU.mult),
                    [e_pow[T], e_xl[T], C0, xn_free[nb]])
                x_free[xs] = e_xnT[T]
                b = tp_bank[T % 2]
                pb = bank(b).bitcast(BF16)
                ev = None
                for k in range(8):
                    ev = S.op("pe", lambda e, k=k, nb=nb, pb=pb: e.transpose(out=pb[:, k * 128:(k + 1) * 128],
                                                                             in_=xnb[nb][:, k * 128:(k + 1) * 128], identity=ident),
                              [e_xnT[T], C0, bank_dep(b)], sig=(k == 7))
                xn_free[nb] = ev
                e_tp[T] = ev

            def a_evac1(T):
                b = tp_bank[T % 2]
                pb = bank(b).bitcast(BF16)
                e_e1[T] = S.op("act", lambda e, T=T, pb=pb: e.activation(out=hnT[:, :, T * 128:(T + 1) * 128],
                                                                           in_=pb.rearrange("p (k t) -> p k t", k=8), func=AF.Copy), [e_tp[T]])
                last_bank_reader[b] = e_e1[T]
                evac_evs.append(e_e1[T])

            def a_evac2(T):
                if not (1 <= T <= 18):
                    return
                m_lo = 28 if T == 1 else 0
                m_hi = 4 if T == 18 else 32
                mm0 = 32 * T - 60 + m_lo
                nm = m_hi - m_lo
                src = hnT[:, :, T * 128:(T + 1) * 128].rearrange("p k (m g) -> p k g m", g=4)[:, :, :, m_lo:m_hi]
                ev2 = S.op("dve", lambda e, src=src, mm0=mm0, nm=nm: e.tensor_copy(out=hnTp[:, :, :, mm0:mm0 + nm], in_=src), [e_e1[T]])
                evac_evs.append(ev2)
                e_e2[T] = ev2

            PF = 5
            for T_ in range(PF):
                a_load(T_)
            for it in range(NA + 4):
                if it + PF < NA:
                    a_load(it + PF)
                if it < NA:
                    a_stat(it)
                if 0 <= it - 1 < NA:
                    a_norm(it - 1)
                if 0 <= it - 2 < NA:
                    a_evac1(it - 2)
                if 0 <= it - 3 < NA:
                    a_evac2(it - 3)
                if it - 3 >= 9:
                    for _ in range(2 if it % 2 == 0 else 1):
                        b1_step(16, 2)
            e_xn = e_xnT[NA - 1]
            A_DONE = list(evac_evs[-3:]) + [last_bank_reader[0], last_bank_reader[1]]
            e_hn_act = [e for e in evac_evs if e[0] == "act"][-1]
            e_hn_dve = ([e for e in evac_evs if e[0] == "dve"] or [None])[-1]
            HN = [e_hn_act, e_hn_dve]
            big_jobs = []
            wc2d = wcv[:].rearrange("p a b -> p (a b)")
            pw2d = pwr[:].rearrange("p a b -> p (a b)")
            e_wc = []
            e_pw = []
            for i_, c0_ in enumerate(range(0, 144 * 128, 2048)):
                big_jobs.append(lambda i_=i_, c0_=c0_: e_wc.append(S.dma("pool", lambda e, c0_=c0_: e.dma_start(out=wc2d[:, c0_:c0_ + 2048], in_=wc_d[:, c0_:c0_ + 2048]), [e_xn, sq_prev[0]], f"wc{i_}")))
            for i_, c0_ in enumerate(range(0, 16 * 512, 2048)):
                big_jobs.append(lambda i_=i_, c0_=c0_: e_pw.append(S.dma("pool", lambda e, c0_=c0_: e.dma_start(out=pw2d[:, c0_:c0_ + 2048], in_=pwr_d[:, c0_:c0_ + 2048]), [], f"pw{i_}")))
            if stage <= 1:
                while big_jobs:
                    big_jobs.pop(0)()
            yield 1, [x for x in HN + A_DONE + (e_wc or []) + (e_pw or []) if x is not None], {"hnT": hnT[:].rearrange("p a b -> p (a b)"), "hnTp": hnTp[:].rearrange("p a b c -> p (a b c)")}


            while b1_state["ev"] < b1_state["mm"]:
                b1_ev()
            rot[0] = [2, 3, 4, 5, 6, 7, 0, 1]
            rpos[0] = 0
            while b1_step(32, 2):
                if big_jobs:
                    big_jobs.pop(0)()
            while b1_state["ev"] < 32:
                b1_ev()
            while big_jobs:
                big_jobs.pop(0)()
            uedge = bass.AP(u_ph, 0, [[16 * UW, 128], [UW, 16], [516, 2], [1, 4]])
            uvb = bass.AP(uvt, 0, [[8, 128], [0, 16], [4, 2], [1, 4]])
            e_uv2 = S.op("dve", lambda e: e.tensor_tensor(out=uedge, in0=uedge, in1=uvb, op=ALU.mult), [u_evs[-1], C0])
            U_DONE = e_uv2
            yield 2, [U_DONE], {"u": u_ph[:].rearrange("p a b -> p (a b)")}

            b_sum = next_bank()
            b_sq = next_bank()
            conv_rot = [next_bank(), next_bank()]
            sq_free = [None, None]
            cv_evs = []
            st_last = None

            def b2_stats(G):
                qi = G % 2
                e_q = S.op("dve", lambda e, G=G, qi=qi: e.tensor_tensor(out=sqs[qi][:], in0=cvs[:, G, :], in1=cvs[:, G, :], op=ALU.mult),
                           [cv_evs[G], sq_free[qi]])
                S.op("pe", lambda e, G=G: e.matmul(bank(b_sum), lhsT=bd32, rhs=cvs[:, G, :], start=(G == 0), stop=(G == 15)),
                     [cv_evs[G], C0, bank_dep(b_sum)], sig=False)
                ev = S.op("pe", lambda e, G=G, qi=qi: e.matmul(bank(b_sq), lhsT=bd32, rhs=sqs[qi][:], start=(G == 0), stop=(G == 15)),
                          [e_q, bank_dep(b_sq)], sig=True)
                sq_free[qi] = ev
                return ev

            for G in range(16):
                bc = conv_rot[G % 2]
                for di in range(9):
                    evp = S.op("pe", lambda e, bc=bc, G=G, di=di: e.matmul(
                        bank(bc), lhsT=wcv[:, G * 9 + di, :], rhs=u_ph[:, G, di:di + 512], start=(di == 0), stop=(di == 8)),
                        [U_DONE, bank_dep(bc), e_xn, e_tp[NA - 1]] + e_wc, sig=(di == 8))
                e_cv = S.op("act", lambda e, bc=bc, G=G: e.activation(out=cvs[:, G, :], in_=bank(bc), func=AF.Identity,
                                                                        bias=vcol("dwb", G), scale=1.0), [evp, C0, e_tp[NA - 1]])
                last_bank_reader[bc] = e_cv
                cv_evs.append(e_cv)
                if G >= 1:
                    st_last = b2_stats(G - 1)
            st_last = b2_stats(15)
            mean, var, lnv, rstd, mr = f32t[0], f32t[1], f32t[2], f32t[3], f32t[4]
            e1 = S.op("dve", lambda e: e.tensor_scalar(out=mean[:], in0=bank(b_sum), scalar1=1.0 / 512, scalar2=None, op0=ALU.mult), [st_last])
            e2 = S.op("dve", lambda e: e.tensor_tensor(out=var[:], in0=mean[:], in1=mean[:], op=ALU.mult), [e1])
            e3 = S.op("dve", lambda e: e.scalar_tensor_tensor(out=var[:], in0=bank(b_sq), scalar=1.0 / 512, in1=var[:],
                                                               op0=ALU.mult, op1=ALU.subtract), [e2, st_last])
            last_bank_reader[b_sum] = e3
            last_bank_reader[b_sq] = e3
            e4 = S.op("act", lambda e: e.activation(out=lnv[:], in_=var[:], func=AF.Ln, bias=epsT, scale=1.0), [e3, e_sm])
            e5 = S.op("act", lambda e: e.activation(out=rstd[:], in_=lnv[:], func=AF.Exp, scale=-0.5), [e4])
            e6 = S.op("dve", lambda e: e.tensor_tensor(out=mr[:], in0=mean[:], in1=rstd[:], op=ALU.mult), [e5, e1])
            s_evs = []
            tfree = [None, None]
            zc_evs = []
            b4 = {"chunk": None}

            def b4_unit(cc, tt):
                if tt == 0:
                    b4["chunk"] = take_chunk()
                sl_, ew_ = b4["chunk"]
                b = next_bank()
                evp = None
                for k in range(8):
                    evp = S.op("pe", lambda e, b=b, sl_=sl_, k=k, tt=tt: e.matmul(
                        bank(b), lhsT=wsl[sl_][:, k, :], rhs=hnT[:, k, HALO + tt * 512:HALO + (tt + 1) * 512],
                        start=(k == 0), stop=(k == 7)), [ew_, bank_dep(b), U_DONE] + HN, sig=(k == 7))
                ez = S.op("act", lambda e, b=b, cc=cc, tt=tt: e.activation(out=zc[:, cc, tt * 512:(tt + 1) * 512], in_=bank(b),
                                                                             func=AF.Silu, bias=vcol("bzc", cc), scale=1.0),
                          [evp, C0, st_last])
                last_bank_reader[b] = ez
                zc_evs.append(ez)
                if tt == 3:
                    release_chunk(sl_, evp)

            for G in range(16):
                ti = G % 2
                tt_ = f32t[5 + ti]
                ea = S.op("dve", lambda e, G=G, tt_=tt_: e.tensor_tensor(out=tt_[:], in0=cvs[:, G, :], in1=rstd[:], op=ALU.mult),
                          [e5, cv_evs[G], st_last, tfree[ti]])
                eb = S.op("dve", lambda e, tt_=tt_: e.tensor_tensor(out=tt_[:], in0=tt_[:], in1=mr[:], op=ALU.subtract), [ea, e6])
                es = S.op("act", lambda e, G=G, tt_=tt_: e.activation(out=cvs[:, G, :], in_=tt_[:], func=AF.Silu,
                                                                        bias=vcol("clnb", G), scale=vcol("clng", G)), [eb, C0])
                tfree[ti] = es
                s_evs.append(es)
                b4_unit(G // 4, G % 4)
            S_DONE = s_evs[-1]
            yield 3, [S_DONE], {"s": cvs[:].rearrange("p a b -> p (a b)")}
            ZC_DONE = zc_evs[-1]

            pw_banks = [next_bank() for _ in range(4)]
            b_stats = [next_bank(), next_bank()]
            y_evs = []
            units = []

            def b5_front(u, cp, h, ev_pw):
                bh = pw_banks[h]
                si = u % 4
                qi = u % 2
                yc, rr = f32t[si * 2], f32t[si * 2 + 1]
                prev = units[u - 4]["ey"] if u >= 4 else None
                eyc = S.op("act", lambda e, bh=bh, yc=yc, cp=cp: e.activation(out=yc[:], in_=bank(bh), func=AF.Identity,
                                                                                bias=vcol("pwb", cp), scale=1.0), [ev_pw, C0, prev, e6, s_evs[-1]])
                last_bank_reader[bh] = eyc
                eys = S.op("pool", lambda e, yc=yc, qi=qi: e.tensor_tensor(out=sqs[qi][:], in0=yc[:], in1=yc[:], op=ALU.mult), [eyc, sq_free[qi]])
                bs = b_stats[qi]
                est = S.op("pe", lambda e, qi=qi, bs=bs: e.matmul(bank(bs), lhsT=bd64s, rhs=sqs[qi][:], start=True, stop=True),
                           [eys, C0, bank_dep(bs)])
                sq_free[qi] = est
                units.append({"cp": cp, "h": h, "eyc": eyc, "est": est, "si": si, "qi": qi})

            def b5_back(u):
                info = units[u]
                cp, h, si, qi = info["cp"], info["h"], info["si"], info["qi"]
                yc, rr = f32t[si * 2], f32t[si * 2 + 1]
                bs = b_stats[qi]
                el = S.op("act", lambda e, rr=rr, bs=bs: e.activation(out=rr[:], in_=bank(bs), func=AF.Ln, bias=epsT, scale=1.0), [info["est"]])
                last_bank_reader[bs] = el
                er = S.op("act", lambda e, rr=rr: e.activation(out=rr[:], in_=rr[:], func=AF.Exp, scale=-0.5), [el])
                et = S.op("dve", lambda e, yc=yc, rr=rr: e.tensor_tensor(out=yc[:], in0=yc[:], in1=rr[:], op=ALU.mult), [info["eyc"], er, info["est"]])
                zsrc = bass.AP(zc, cp * TM + h, [[4 * TM, 128], [4, 512]])
                ydst = bass.AP(yT, cp * TM + h, [[8 * TM, 128], [4, 512]])
                ey = S.op("dve", lambda e, yc=yc, cp=cp, zsrc=zsrc, ydst=ydst: e.scalar_tensor_tensor(
                    out=ydst, in0=yc[:], scalar=vcol("gcg", cp), in1=zsrc, op0=ALU.mult, op1=ALU.mult),
                    [et, ZC_DONE, C0, u_evs[-1]])
                info["ey"] = ey
                y_evs.append(ey)

            e_wv = S.dma("pool", lambda e: e.dma_start(out=wv[:], in_=win_d[:, 2560:3072].rearrange("(k p) c -> p k c", p=128)), [st_last], "wv")
            v_banks = [b for b in range(8) if b not in pw_banks and b not in b_stats]
            v_evs = []

            def v_unit(T):
                b = v_banks[T % len(v_banks)]
                evp = None
                for k in range(8):
                    evp = S.op("pe", lambda e, b=b, k=k, T=T: e.matmul(bank(b), lhsT=hnT[:, k, T * 128:(T + 1) * 128], rhs=wv[:, k, :],
                                                                       start=(k == 0), stop=(k == 7)), [e_wv, bank_dep(b), st_last] + HN, sig=(k == 7))
                evv = S.op("dve", lambda e, b=b, T=T: e.tensor_tensor(out=vp[:, T, :, :].rearrange("p a b -> p (a b)"), in0=bank(b), in1=bvb[:], op=ALU.add),
                           [evp, C0, st_last])
                last_bank_reader[b] = evv
                v_evs.append(evv)

            vT = [0]

            def v_some(n):
                for _ in range(n):
                    if vT[0] < NTILE:
                        v_unit(vT[0])
                        vT[0] += 1

            u = 0
            for cp in range(4):
                evh = [None] * 4
                for G in range(16):
                    for h in range(4):
                        evh[h] = S.op("pe", lambda e, cp=cp, G=G, h=h: e.matmul(
                            bank(pw_banks[h]), lhsT=pwr[32 * h:32 * h + 32, G, cp * 128:(cp + 1) * 128],
                            rhs=cvs[32 * h:32 * h + 32, G, :], start=(G == 0), stop=(G == 15), tile_position=(32 * h, 0)),
                            [S_DONE, bank_dep(pw_banks[h])] + e_pw, sig=(G == 15))
                for h in range(4):
                    b5_front(u, cp, h, evh[h])
                    if u >= 1:
                        b5_back(u - 1)
                    u += 1
                    v_some(1 if h < 3 else 2)
            b5_back(u - 1)
            v_some(NTILE)
            V_DONE = v_evs[-1]
            er = None
            YC_DONE = y_evs[-1]
            yield 4, [YC_DONE], {"yT": yT[:].rearrange("p a b -> p (a b)"), "zc": zc[:].rearrange("p a b -> p (a b)")}

            bar_pe = S.op("pe", lambda e: e.matmul(bank(b_stats[0])[:, 0:8], lhsT=ident, rhs=ident[:, 0:8], start=True, stop=True),
                          [YC_DONE, bank_dep(b_stats[0]), bank_dep(b_stats[1])])
            bar_act = S.op("act", lambda e: e.activation(out=smalls[:, 10:11], in_=smalls[:, 9:10], func=AF.Copy), [YC_DONE, bar_pe])
            bar_dve = S.op("dve", lambda e: e.tensor_copy(out=smalls[:, 11:12], in_=smalls[:, 9:10]), [bar_act, bar_pe])
            BAR = [bar_pe, bar_act, bar_dve]
            last_bank_reader[:] = [bar_dve] * 8

            e_tt = big_dma("pool", ttx[:].rearrange("p a b -> p (a b)"), ttx_d, BAR, "tt", step=1792)
            e_rv = S.dma("pool", lambda e: e.dma_start(out=rvt[:].rearrange("p a b -> p (a b)"), in_=rv_d), BAR, "rv")
            e_wos = []
            for nh_ in range(2):
                e_wo = S.dma("pool", lambda e, nh_=nh_: e.dma_start(out=wo[:, :, nh_ * 512:(nh_ + 1) * 512], in_=wout_d[:, nh_ * 512:(nh_ + 1) * 512].rearrange("(k p) c -> p k c", p=128)), BAR, f"wo{nh_}")
                e_wos.append(e_wo)
            e_tx = None
            e_rw = S.dma("pool", lambda e: e.dma_start(out=rwt[:], in_=rw_d), BAR, "rw")
            e_rvf = None
            for cls_, qb_ in ((0, 0), (1, 7)):
                e_rvf = S.op("dve", lambda e, cls_=cls_, qb_=qb_: e.tensor_copy(
                    out=rvf[:, cls_, :].rearrange("p (a c) -> p a c", a=24), in_=rvt[:, qb_, :].unsqueeze(2).to_broadcast([128, 24, 64])),
                    BAR + [e_rv, e_rvf])
            e_z1 = S.op("pool", lambda e: e.memset(qzA[64:128, :], 0.0), BAR)
            e_z2 = S.op("pool", lambda e: e.memset(qzB[0:64, :], 0.0), BAR + [e_z1])
            e_pz = e_z2
            for i_ in range(4):
                e_pz = S.op("pool", lambda e, i_=i_: e.memset(pT[i_][:, 0:192], 0.0), BAR + [e_pz])
                e_pz = S.op("pool", lambda e, i_=i_: e.memset(pT[i_][:, 1408:1536], 0.0), BAR + [e_pz])

            SB_A, SB_B, B_OS, B_ST = 0, 3, 6, 7
            pair_done = None
            pT_free = [None, None, None, None]
            st_state = {"os_free": None, "st_free": None, "sq_free": [None, None, None]}
            tw_free = None
            sA_free = None
            sB_free = None
            os_free = None
            st_free = None
            att_tail = []
            for pp in range(int(os.environ.get('MK_NPP', 4))):
                sq_, eq_ = take_chunk()
                sk_, ek_ = take_chunk()
                sz_, ez_ = take_chunk()
                if os.environ.get("MK_C1", "") == "wonly" and pp >= 1:
                    release_chunk(sq_, None); release_chunk(sk_, None); release_chunk(sz_, None)
                    continue
                proj_banks = [0, 1, 2, 3, 4, 5]
                pbi = [0]

                def nb_():
                    b = proj_banks[pbi[0] % 6]
                    pbi[0] += 1
                    return b
                extra = [pair_done] if pair_done is not None else []
                q_evs = []
                NKQ = int(os.environ.get("MK_NKQ", 8)) if pp >= 1 else 8
                for tt in range(4 if NKQ == 8 or pp == 0 else 1):
                    b = nb_()
                    for k in range(NKQ):
                        evp = S.op("pe", lambda e, b=b, k=k, tt=tt, sq_=sq_: e.matmul(
                            bank(b), lhsT=wsl[sq_][:, k, :], rhs=hnT[:, k, HALO + tt * 512:HALO + (tt + 1) * 512],
                            start=(k == 0), stop=(k == 7)), [eq_, bank_dep(b)] + HN + BAR, sig=(k == NKQ - 1))
                    if os.environ.get("MK_C1", "") == "qafter" and pp >= 1:
                        continue
                    if "qev" in SKIP and pp >= 1:
                        last_bank_reader[b] = evp
                        q_evs.append(evp)
                        continue
                    e_qa = S.op("act", lambda e, b=b, tt=tt, pp=pp: e.activation(out=qzA[0:64, tt * 512:(tt + 1) * 512], in_=ps[0:64, b * 512:(b + 1) * 512],
                                                                                   func=AF.Identity, bias=bq8[0:64, pp:pp + 1], scale=0.125), [evp, e_bq8] + BAR + extra)
                    e_qb = S.op("act", lambda e, b=b, tt=tt, pp=pp: e.activation(out=qzB[64:128, tt * 512:(tt + 1) * 512], in_=ps[64:128, b * 512:(b + 1) * 512],
                                                                                   func=AF.Identity, bias=bq8[64:128, pp:pp + 1], scale=0.125), [evp, e_bq8, e_z2] + BAR + extra)
                    last_bank_reader[b] = e_qb
                    q_evs.append(e_qb)
                if os.environ.get("MK_C1", "") == "qafter" and pp >= 1:
                    break
                release_chunk(sq_, evp)
                k_evs = []
                for tt in range(5):
                    b = nb_()
                    for k in range(8):
                        evp = S.op("pe", lambda e, b=b, k=k, tt=tt, sk_=sk_: e.matmul(
                            bank(b), lhsT=wsl[sk_][:, k, :], rhs=hnT[:, k, tt * 512:(tt + 1) * 512],
                            start=(k == 0), stop=(k == 7)), [ek_, bank_dep(b)] + HN + BAR, sig=(k == 7))
                    if "kev" in SKIP and pp >= 1:
                        last_bank_reader[b] = evp
                        k_evs.append(evp)
                        continue
                    e_k = S.op("dve", lambda e, b=b, tt=tt, pp=pp: e.tensor_scalar(out=kT[:, tt * 512:(tt + 1) * 512], in0=bank(b), scalar1=vcol("bk", pp),
                                                                                     scalar2=None, op0=ALU.add), [evp, C0] + BAR + extra)
                    last_bank_reader[b] = e_k
                    k_evs.append(e_k)
                release_chunk(sk_, evp)
                e_tx = S.op("act", lambda e, pp=pp: e.activation(out=ttx[:, 2 * pp:2 * pp + 2, :], in_=ttx[:, 2 * pp:2 * pp + 2, :], func=AF.Exp), e_tt + BAR)
                QK = [q_evs[-1], q_evs[-2], k_evs[-1], V_DONE, e_tx, e_rv]
                if os.environ.get("MK_C1", "") == "projafter" and pp >= 1:
                    continue

                if os.environ.get("MK_C1", "") == "proj":
                    yield 6, QK, {"kT": kT[:], "qzA": qzA[:], "qzB": qzB[:], "ttx": ttx[:].rearrange("p a b -> p (a b)")}

                e_tw = None
                for a_ in range(2):
                    e_tw = S.op("dve", lambda e, a_=a_, pp=pp: e.tensor_tensor(out=ttw[:, a_, :], in0=ttx[:, 2 * pp + a_, :], in1=rwt[:], op=ALU.mult),
                                [e_tx, e_rw, tw_free, e_tw])
                blk = {}

                def att_S(qb):
                    interior = 1 <= qb <= 6
                    info = {"masks": [], "pi": []}
                    for hd, (SB, qz) in enumerate(((SB_A, qzA), (SB_B, qzB))):
                        for sl in range(6):
                            kt = 5 - sl
                            lo, hi = (0, 4)
                            if interior:
                                lo, hi = {0: (3, 4), 5: (0, 2)}.get(sl, (0, 4))
                            c0 = SB * 512 + sl * 256
                            evp = S.op("pe", lambda e, c0=c0, lo=lo, hi=hi, kt=kt, qb=qb, qz=qz: e.matmul(
                                ps[:, c0 + lo * 64:c0 + hi * 64],
                                lhsT=kT[:, (2 * qb + kt) * 128:(2 * qb + kt + 1) * 128], rhs=qz[:, qb * 256 + lo * 64:qb * 256 + hi * 64],
                                start=True, stop=True), QK + [bank_dep(SB), bank_dep(SB + 1), bank_dep(SB + 2)], sig=(sl == 5))
                        pi = (qb % 2) * 2 + hd
                        x0, x1 = (192, 1408) if interior else (0, 1536)
                        e_ex = S.op("act", lambda e, SB=SB, pi=pi, x0=x0, x1=x1: e.activation(
                            out=pT[pi][:, x0:x1], in_=ps[:, SB * 512 + x0:SB * 512 + x1], func=AF.Exp), [evp, pT_free[pi], e_pz])
                        for bb_ in range(3):
                            last_bank_reader[SB + bb_] = e_ex
                        info["last_ex"] = e_ex
                        h = 2 * pp + hd
                        pview = pT[pi][:].rearrange("p (a b c) -> p a b c", a=6, b=4)
                        if interior:
                            tview = bass.AP(ttw, hd * 896, [[2 * 896, 128], [128, 6], [64, 4], [1, 64]])
                            e_mk = S.op("dve", lambda e, pview=pview, tview=tview: e.tensor_tensor(out=pview, in0=pview, in1=tview, op=ALU.mult), [e_ex, e_tw])
                        else:
                            tview = bass.AP(ttx, h * 896, [[8 * 896, 128], [128, 6], [64, 4], [1, 64]])
                            e_m1_ = S.op("dve", lambda e, pview=pview, tview=tview: e.tensor_tensor(out=pview, in0=pview, in1=tview, op=ALU.mult), [e_ex, e_tx])
                            cls_ = 0 if qb == 0 else 1
                            e_mk = S.op("dve", lambda e, pi=pi, cls_=cls_: e.tensor_tensor(out=pT[pi][:], in0=pT[pi][:], in1=rvf[:, cls_, :], op=ALU.mult), [e_m1_, e_rvf])
                        info["masks"].append(e_mk)
                        info["pi"].append(pi)
                    blk[qb] = info

                def att_PV(qb):
                    nonlocal_os = st_state
                    interior = 1 <= qb <= 6
                    info = blk[qb]
                    (piA, piB) = info["pi"]
                    deps = info["masks"] + [V_DONE, st_state["os_free"], e_sm]
                    e_pv = None
                    o0 = B_OS * 512

                    def rng(sl):
                        return (0, 4)
                    for sl in range(6):
                        Tk = 2 * qb + 5 - sl
                        lo, hi = rng(sl)
                        first = (sl == 0)
                        lastk = (sl == 5)
                        S.op("pe", lambda e, sl=sl, Tk=Tk, lo=lo, hi=hi, first=first, lastk=lastk, pp=pp, piA=piA: e.matmul(
                            ps[0:64, o0 + lo * 64:o0 + hi * 64], lhsT=vp[:, Tk, pp, 0:64], rhs=pT[piA][:, sl * 256 + lo * 64:sl * 256 + hi * 64],
                            start=first, stop=lastk, tile_position=(0, 0)), deps, sig=False)
                        S.op("pe", lambda e, sl=sl, Tk=Tk, lo=lo, hi=hi, first=first, lastk=lastk, pp=pp, piB=piB: e.matmul(
                            ps[64:128, o0 + lo * 64:o0 + hi * 64], lhsT=vp[:, Tk, pp, 64:128], rhs=pT[piB][:, sl * 256 + lo * 64:sl * 256 + hi * 64],
                            start=first, stop=lastk, tile_position=(0, 64)), deps, sig=False)
                    for sl in range(6):
                        lo, hi = rng(sl)
                        lastk = (sl == 5)
                        S.op("pe", lambda e, sl=sl, lo=lo, hi=hi, lastk=lastk, piA=piA: e.matmul(
                            ps[0:64, o0 + 256 + lo * 64:o0 + 256 + hi * 64], lhsT=ones64[:], rhs=pT[piA][:, sl * 256 + lo * 64:sl * 256 + hi * 64],
                            start=(sl == 0), stop=lastk, tile_position=(0, 0)), deps, sig=False)
                        e_pv = S.op("pe", lambda e, sl=sl, lo=lo, hi=hi, lastk=lastk, piB=piB: e.matmul(
                            ps[64:128, o0 + 256 + lo * 64:o0 + 256 + hi * 64], lhsT=ones64[:], rhs=pT[piB][:, sl * 256 + lo * 64:sl * 256 + hi * 64],
                            start=(sl == 0), stop=lastk, tile_position=(0, 64)), deps, sig=lastk)
                    pT_free[piA] = e_pv
                    pT_free[piB] = e_pv
                    oi = qb % 3
                    e_s2 = S.op("act", lambda e, oi=oi: e.activation(out=ess[oi][:], in_=ps[:, B_OS * 512 + 256:B_OS * 512 + 512], func=AF.Square, scale=1e-3),
                                [e_pv, st_state["sq_free"][oi]])
                    e_oc = S.op("dve", lambda e, qb=qb: e.tensor_copy(out=oS[:, qb * 256:(qb + 1) * 256], in_=ps[:, B_OS * 512:B_OS * 512 + 256]),
                                [e_pv, pair_done, e_s2])
                    st_state["os_free"] = e_oc
                    e_o2 = S.op("dve", lambda e, qb=qb, oi=oi: e.tensor_tensor(out=osq[oi][:], in0=oS[:, qb * 256:(qb + 1) * 256], in1=oS[:, qb * 256:(qb + 1) * 256], op=ALU.mult),
                                [e_oc, st_state["sq_free"][oi]])
                    info["e_s2"] = e_s2
                    info["e_o2"] = e_o2
                    info["e_oc"] = e_oc

                def att_stat(qb):
                    info = blk[qb]
                    oi = qb % 3
                    S.op("pe", lambda e, oi=oi: e.matmul(ps[:, B_ST * 512:B_ST * 512 + 256], lhsT=bd64s, rhs=osq[oi][:], start=True, stop=False),
                         [info["e_o2"], C0, st_state["st_free"]], sig=False)
                    e_stm = S.op("pe", lambda e, oi=oi: e.matmul(ps[:, B_ST * 512:B_ST * 512 + 256], lhsT=ident, rhs=ess[oi][:], start=False, stop=True),
                                 [info["e_s2"], C0], sig=True)
                    st_state["sq_free"][oi] = e_stm
                    info["e_stm"] = e_stm

                def att_stcopy(qb):
                    info = blk[qb]
                    ev = S.op("act", lambda e, qb=qb: e.activation(out=stS[:, qb * 256:(qb + 1) * 256], in_=ps[:, B_ST * 512:B_ST * 512 + 256], func=AF.Copy),
                              [info["e_stm"], pair_done])
                    st_state["st_free"] = ev
                    return ev

                NQ = 8
                att_S(0)
                last_sc = None
                for n in range(NQ):
                    if n >= 2:
                        last_sc = att_stcopy(n - 2)
                    if n + 1 < NQ:
                        att_S(n + 1)
                    att_PV(n)
                    if n >= 1:
                        att_stat(n - 1)
                if NQ >= 2:
                    last_sc = att_stcopy(NQ - 2)
                att_stat(NQ - 1)
                last_sc = att_stcopy(NQ - 1)
                e_ex = blk[NQ - 1]["last_ex"]
                z_evs = []
                for tt in range(4):
                    b = nb_()
                    for k in range(8):
                        evp = S.op("pe", lambda e, b=b, k=k, tt=tt, sz_=sz_: e.matmul(
                            bank(b), lhsT=wsl[sz_][:, k, :], rhs=hnT[:, k, HALO + tt * 512:HALO + (tt + 1) * 512],
                            start=(k == 0), stop=(k == 7)), [ez_, bank_dep(b)] + HN + BAR, sig=(k == 7))
                    if "zev" in SKIP and pp >= 1:
                        last_bank_reader[b] = evp
                        z_evs.append(evp)
                        continue
                    e_z = S.op("act", lambda e, b=b, tt=tt, pp=pp: e.activation(out=za[:, tt * 512:(tt + 1) * 512], in_=bank(b), func=AF.Silu,
                                                                                  bias=vcol("bza", pp), scale=1.0), [evp, C0] + BAR + extra)
                    last_bank_reader[b] = e_z
                    z_evs.append(e_z)
                release_chunk(sz_, evp)
                e_l = S.op("act", lambda e: e.activation(out=stS[:], in_=stS[:], func=AF.Ln), [last_sc])
                e_r = S.op("act", lambda e: e.activation(out=stS[:], in_=stS[:], func=AF.Exp, scale=-0.5), [e_l])
                e_t = S.op("dve", lambda e: e.tensor_tensor(out=oS[:], in0=oS[:], in1=stS[:], op=ALU.mult), [e_r, blk[NQ - 1]["e_oc"], blk[NQ - 1]["e_o2"]])
                e_y = S.op("dve", lambda e, pp=pp: e.scalar_tensor_tensor(out=yT[:, 4 + pp, :], in0=oS[:], scalar=vcol("gag", pp), in1=za[:],
                                                                           op0=ALU.mult, op1=ALU.mult), [e_t, z_evs[-1], C0])
                for _d in range(int(os.environ.get("MK_PAD", 0))):
                    e_y = S.op("dve", lambda e: e.tensor_copy(out=smalls[:, 12:13], in_=smalls[:, 9:10]), [e_y])
                pair_done = e_y
                tw_free = e_y
            ATT_DONE = pair_done
            yield 6, [ATT_DONE], {"yT": yT[:].rearrange("p a b -> p (a b)"), "oS": oS[:], "stS": stS[:], "za": za[:]}

            e_fg = S.dma("sp", lambda e: e.dma_start(out=lgfg[:], in_=fg_d), [ATT_DONE, e_xn], "fg")
            barD_pe = S.op("pe", lambda e: e.matmul(bank(7)[:, 0:8], lhsT=ident, rhs=ident[:, 0:8], start=True, stop=True), [ATT_DONE, st_free, os_free, e_ex])
            barD_act = S.op("act", lambda e: e.activation(out=smalls[:, 10:11], in_=smalls[:, 9:10], func=AF.Copy), [ATT_DONE, barD_pe])
            barD = [ATT_DONE, barD_pe, barD_act]
            junkDP = ps[:, 6 * 512:8 * 512]
            xr_free = [None, None, None]
            ht_free = [None, None, None]
            ot_free = [None, None]
            dbanks = [(0, 1), (2, 3), (4, 5)]
            d_free = [barD_act] * 3
            out_evs = []
            e_h = [None] * 16
            e_ssq = [None] * 16
            e_pw2 = [None] * 16

            def d_main(T):
                i2 = T % 2
                i3 = T % 3
                e_xr = S.dma("sp", lambda e, T=T, i3=i3: e.dma_start(out=xrs[i3][:], in_=x_d[HALO + T * 128:HALO + (T + 1) * 128, :]),
                             barD + [xr_free[i3]], f"xr{i3}")
                bi = T % 3
                b0 = dbanks[bi][0]
                evp = None
                for nh in range(2):
                    for k in range(8):
                        evp = S.op("pe", lambda e, b0=b0, nh=nh, k=k, T=T: e.matmul(
                            bank(b0 + nh), lhsT=yT[:, k, T * 128:(T + 1) * 128], rhs=wo[:, k, nh * 512:(nh + 1) * 512],
                            start=(k == 0), stop=(k == 7)), barD + e_wos + [d_free[bi], YC_DONE], sig=(k == 7 and nh == 1))
                e_h[T] = S.op("dve", lambda e, b0=b0, i2=i2, i3=i3: e.tensor_tensor(out=hts[i3][:], in0=ps[:, b0 * 512:b0 * 512 + 1024], in1=xrs[i3][:], op=ALU.add),
                              [evp, e_xr, ht_free[i3]])
                d_free[bi] = e_h[T]
                xr_free[i3] = e_h[T]
                e_ssq[T] = S.op("act", lambda e, T=T, i3=i3: e.activation(out=junkDP, in_=hts[i3][:], func=AF.Square, accum_out=ssD[:, T:T + 1]), [e_h[T], barD_act, e_ssq[T - 1] if T >= 1 else None])

            def d_rs(T):
                e_r1 = S.op("dve", lambda e, T=T: e.tensor_scalar(out=rsD[:, T:T + 1], in0=ssD[:, T:T + 1], scalar1=1.0 / D, scalar2=EPS,
                                                                   op0=ALU.mult, op1=ALU.add), [e_ssq[T]])
                e_pw2[T] = S.op("pool", lambda e, T=T: e.tensor_tensor(out=rsD[:, T:T + 1], in0=rsD[:, T:T + 1], in1=mhalf, op=ALU.pow), [e_r1])

            def d_out(T):
                i2 = T % 2
                i3 = T % 3
                e_o = S.op("dve", lambda e, T=T, i2=i2, i3=i3: e.scalar_tensor_tensor(out=ots[i2][:], in0=hts[i3][:], scalar=rsD[:, T:T + 1], in1=lgfg[:],
                                                                                       op0=ALU.mult, op1=ALU.mult), [e_pw2[T], e_fg, ot_free[i2]])
                ht_free[i3] = e_o
                e_st = S.dma("act", lambda e, T=T, i2=i2: e.dma_start(out=out_d[T * 128:(T + 1) * 128, :], in_=ots[i2][:]), [e_o], f"o{i2}")
                ot_free[i2] = e_st
                out_evs.append(e_st)

            for it in range(16 + 2):
                if it < 16:
                    d_main(it)
                if 0 <= it - 1 < 16:
                    d_rs(it - 1)
                if 0 <= it - 2 < 16:
                    d_out(it - 2)
            fin = out_evs[-2:]
            yield 99, fin, {}

        final = None
        for stg, fdeps, dumps in prog():
            if stg >= stage:
                final = (fdeps, dumps)
                break
        fdeps, dumps = final
        fin = list(fdeps)
        for nm, src in dumps.items():
            dd_ = nc.dram_tensor('dbg_' + nm, [128, int(np.prod(src.shape[1:]))], src.dtype, kind='ExternalOutput').ap()
            n_ = dd_.shape[1]
            for c0 in range(0, n_, 2048):
                c1 = min(n_, c0 + 2048)
                ev_ = S.dma('sp', lambda e, dd_=dd_, src=src, c0=c0, c1=c1: e.dma_start(out=dd_[:, c0:c1], in_=src[:, c0:c1]), list(fdeps), 'dbg_' + nm)
            fin.append(ev_)
        S.wait("sp", fin)
        S.emit(nc, st)
    return nc


def _host_consts():
    ident = np.eye(128, dtype=np.float32)
    p = np.arange(128)
    bd32 = (p[:, None] // 32 == p[None, :] // 32).astype(np.float32)
    bd64s = (p[:, None] // 64 == p[None, :] // 64).astype(np.float32) / 64.0
    return np.concatenate([ident, bd32, bd64s], axis=1).astype(ml_dtypes.bfloat16)


def _ph(vec512):
    a = np.asarray(vec512, np.float32).reshape(16, 32)
    return np.tile(a.T, (4, 1))


def _cm(vec, n):
    return np.ascontiguousarray(np.asarray(vec, np.float32).reshape(n, 128).T)


def _shared_layouts(inp):
    w_in = np.ascontiguousarray(inp["w_in"][0], dtype=np.float32)
    b_in = np.asarray(inp["b_in"][0], np.float32)
    vecs = np.zeros((128, NVEC), np.float32)

    def put(name, arr):
        o, w = VC[name]
        assert arr.shape == (128, w), (name, arr.shape)
        vecs[:, o:o + w] = arr
    put("bua", _ph(b_in[0:512]))
    put("bub", _ph(b_in[512:1024]))
    put("bzc", _cm(b_in[1024:1536], 4))
    put("bq", _cm(b_in[1536:2048], 4))
    put("bk", _cm(b_in[2048:2560], 4))
    put("bza", _cm(b_in[3072:3584], 4))
    put("dwb", _ph(inp["dw_b"][0]))
    put("clng", _ph(inp["cln_g"][0]))
    put("clnb", _ph(inp["cln_b"][0]))
    put("pwb", _cm(inp["pw_b"][0], 4))
    put("gcg", _cm(inp["gn_conv_g"][0], 4))
    put("gag", _cm(inp["gn_att_g"][0], 4))
    bv_b = np.ascontiguousarray(np.broadcast_to(b_in[2560:3072][None, :], (128, 512)), dtype=np.float32)
    lng_b = np.ascontiguousarray(np.broadcast_to(np.asarray(inp["ln_g"][0], np.float32)[None, :], (128, D)))
    fg_b = np.ascontiguousarray(np.broadcast_to(np.asarray(inp["final_g"], np.float32)[None, :], (128, D)))
    pw = np.asarray(inp["pw_w"][0], np.float32).reshape(16, 32, 512)
    pw_rep = np.tile(pw.transpose(1, 0, 2), (4, 1, 1)).reshape(128, 16 * 512)
    dw = np.asarray(inp["dw_w"][0], np.float32)
    wc = np.zeros((4, 32, 16, 9, 4, 32), np.float32)
    cidx = np.arange(32)
    for g in range(4):
        for h in range(4):
            for di in range(9):
                j = 4 * (di - 4) + g - h + 15
                if 0 <= j <= 30:
                    for G in range(16):
                        wc[g, cidx, G, di, h, cidx] = dw[j, G * 32:(G + 1) * 32]
    wconv = np.ascontiguousarray(wc.reshape(128, 144 * 128))
    rpb = np.asarray(inp["rpb"][0], np.float32)
    c = np.arange(64)
    w = np.arange(64)
    cs = np.clip(w - 8, 0, 48)
    colok = (c[:, None] >= cs[None, :]) & (c[:, None] < cs[None, :] + 16)
    crel = np.clip(c[:, None] - w[None, :] + 15, 0, 30)
    ttxb = np.full((2, 64, 8, 14, 64), NEG, np.float32)
    for i2 in range(2):
        for e in range(14):
            delta = 6 - e + i2
            for h in range(8):
                vals = rpb[h, delta + 7][crel]
                ttxb[i2, :, h, e, :] = np.where(colok, vals, NEG)
    ttxb = np.ascontiguousarray(ttxb.reshape(128, 8 * 896))
    rwin = np.zeros((2, 64, 14), np.float32)
    for i2 in range(2):
        for e in range(14):
            if -4 <= 6 - e + i2 <= 3:
                rwin[i2, :, e] = 1.0
    rwin = np.ascontiguousarray(np.repeat(rwin.reshape(128, 14, 1), 64, axis=2).reshape(128, 896))
    return dict(w_in=w_in, w_out=np.ascontiguousarray(inp["w_out"][0], dtype=np.float32), pw_rep=np.ascontiguousarray(pw_rep),
                wconv=wconv, constb=_host_consts(), vecs=vecs, bv_b=bv_b, lng_b=lng_b, fg_b=fg_b, ttxb=ttxb, rwin=rwin)


def _core_layouts(inp, core):
    b, j = core // 4, core % 4
    x = np.asarray(inp["x"], np.float32)
    lo = j * TM - HALO
    x_ext = np.zeros((NT, D), np.float32)
    s0, s1 = max(lo, 0), min(lo + NT, SEQ)
    x_ext[s0 - lo:s1 - lo] = x[b, s0:s1]
    R0 = 32 * j
    rv = np.zeros((2, 64, 8, 6, 4), np.float32)
    for qb in range(8):
        for sl in range(6):
            kt = 5 - sl
            for r4 in range(4):
                r = R0 + 4 * qb + r4
                rs = min(max(r - 4, 0), 120)
                for i2 in range(2):
                    kr = R0 - 4 + 4 * qb + 2 * kt + i2
                    if rs <= kr < rs + 8:
                        rv[i2, :, qb, sl, r4] = 1.0
    rv = np.ascontiguousarray(rv.reshape(128, 8 * 24))
    uvalid = np.ones((128, 8), np.float32)
    if j == 0:
        uvalid[:, 0:4] = 0.0
    if j == 3:
        uvalid[:, 4:8] = 0.0
    return dict(x_ext=x_ext, rv=rv, uvalid=uvalid)


_NC_CACHE = {}


def kernel(**inputs):
    shared = _shared_layouts(inputs)
    in_maps = []
    for core in range(NCORE):
        m = dict(shared)
        m.update(_core_layouts(inputs, core))
        in_maps.append(m)
    if DEBUG not in _NC_CACHE:
        _NC_CACHE[DEBUG] = build_nc(debug=DEBUG)
    nc = _NC_CACHE[DEBUG]
    res = run_bass_kernel_spmd(nc, in_maps, core_ids=list(range(NCORE)))
    out = np.empty((2, SEQ, D), np.float32)
    for core in range(NCORE):
        b, j = core // 4, core % 4
        out[b, j * TM:(j + 1) * TM] = res.results[core]["out"]
    if DEBUG:
        kernel.last_results = res.results
    return out
```

```python
import os
from contextlib import ExitStack

import numpy as np
import ml_dtypes

import concourse.bass as bass
import concourse.mybir as mybir
from concourse.bass_utils import run_bass_kernel_spmd

F32 = mybir.dt.float32
BF16 = mybir.dt.bfloat16
AF = mybir.ActivationFunctionType
ALU = mybir.AluOpType

D = 1024
SEQ = 8192
NCORE = 8
TM = 2048
HALO = 256
NT = TM + 2 * HALO
NTILE = NT // 128
PROJ = 3584
EPS = 1e-6
UW = 520
NEG = -30000.0

ENGS = ["pe", "act", "dve", "pool", "sp"]


class Sched:
    def __init__(self):
        self.ops = {e: [] for e in ENGS}
        self.cnt = {e: 0 for e in ENGS}
        self.dcnt = {}
        self.pool_fifo = []
        self.max_swdge = 6

    def op(self, eng, fn, deps=(), sig=True):
        ev = None
        if sig:
            self.cnt[eng] += 1
            ev = (eng, self.cnt[eng])
        self.ops[eng].append((fn, [d for d in deps if d is not None], sig, None))
        return ev

    def dma(self, eng, fn, deps, sem):
        self.dcnt[sem] = self.dcnt.get(sem, 0) + 16
        ev = ("d:" + sem, self.dcnt[sem])
        deps = [d for d in deps if d is not None]
        if eng == "pool":
            if len(self.pool_fifo) >= self.max_swdge:
                deps.append(self.pool_fifo.pop(0))
            self.pool_fifo.append(ev)
        self.ops[eng].append((fn, deps, False, "d:" + sem))
        return ev

    def wait(self, eng, deps):
        self.ops[eng].append((None, [d for d in deps if d is not None], False, None))

    def emit(self, nc, stack):
        names = list(ENGS) + ["d:" + s for s in self.dcnt]
        sems = {}
        for n in names:
            sems[n] = stack.enter_context(nc.semaphore("s_" + n.replace(":", "_")))
        block = stack.enter_context(nc.Block())
        ops = self.ops

        def run(engname, engine):
            seen = {}
            for fn, deps, sig, dsem in ops[engname]:
                need = {}
                for (src, v) in deps:
                    if v > seen.get(src, 0):
                        need[src] = max(need.get(src, 0), v)
                for src, v in need.items():
                    engine.wait_ge(sems[src], v)
                    seen[src] = v
                if fn is None:
                    continue
                ins = fn(engine)
                if sig:
                    ins.then_inc(sems[engname], 1)
                elif dsem is not None:
                    ins.then_inc(sems[dsem], 16)

        @block.tensor
        def _(e):
            run("pe", e)

        @block.scalar
        def _(e):
            run("act", e)

        @block.vector
        def _(e):
            run("dve", e)

        @block.gpsimd
        def _(e):
            run("pool", e)

        @block.sync
        def _(e):
            run("sp", e)


VC = {}
_o = 0
for _n, _w in [("bua", 16), ("bub", 16), ("bzc", 4), ("bq", 4), ("bk", 4), ("bza", 4),
               ("dwb", 16), ("clng", 16), ("clnb", 16), ("pwb", 4), ("gcg", 4), ("gag", 4)]:
    VC[_n] = (_o, _w)
    _o += _w
NVEC = _o

DEBUG = bool(int(os.environ.get("MK_DEBUG", "0")))


def build_nc(debug=False, stage=99):
    nc = bass.Bass("TRN2", target_bir_lowering=False)
    dt = nc.dram_tensor
    x_d = dt("x_ext", [NT, D], F32, kind="ExternalInput").ap()
    win_d = dt("w_in", [D, PROJ], F32, kind="ExternalInput").ap()
    wout_d = dt("w_out", [D, D], F32, kind="ExternalInput").ap()
    pwr_d = dt("pw_rep", [128, 16 * 512], F32, kind="ExternalInput").ap()
    wc_d = dt("wconv", [128, 144 * 128], F32, kind="ExternalInput").ap()
    cb_d = dt("constb", [128, 3 * 128], BF16, kind="ExternalInput").ap()
    vec_d = dt("vecs", [128, NVEC], F32, kind="ExternalInput").ap()
    bv_d = dt("bv_b", [128, 512], F32, kind="ExternalInput").ap()
    lng_d = dt("lng_b", [128, D], F32, kind="ExternalInput").ap()
    fg_d = dt("fg_b", [128, D], F32, kind="ExternalInput").ap()
    ttx_d = dt("ttxb", [128, 8 * 896], F32, kind="ExternalInput").ap()
    rv_d = dt("rv", [128, 8 * 24], F32, kind="ExternalInput").ap()
    uv_d = dt("uvalid", [128, 8], F32, kind="ExternalInput").ap()
    rw_d = dt("rwin", [128, 896], F32, kind="ExternalInput").ap()
    out_d = dt("out", [TM, D], F32, kind="ExternalOutput").ap()
    S = Sched()
    with ExitStack() as st:
        cur = [(nc.sbuf_base + 63) // 64 * 64]
        top = nc.sbuf_top
        OFF = {}

        def alloc(name, shape, dtype, at=None):
            nbytes = int(np.prod(shape[1:])) * (2 if dtype == BF16 else 4)
            nbytes = (nbytes + 63) // 64 * 64
            if at is None:
                off = cur[0]
                cur[0] += nbytes
            else:
                off = at
            assert off + nbytes <= top, (name, off, nbytes, top)
            t = nc.alloc_sbuf_tensor_at(name, list(shape), dtype, offset=off)
            OFF[name] = off
            return t

        hnT = alloc("hnT", [128, 8, NT], BF16)
        regY = cur[0]
        hnTp = alloc("hnTp", [128, 8, 4, UW], BF16)
        yT = alloc("yT", [128, 8, TM], BF16, at=regY)
        cst = alloc("cst", [128, 3, 128], BF16)
        ones64 = alloc("ones64", [128, 64], BF16)
        vecs = alloc("vecs", [128, NVEC], F32)
        bq8 = alloc("bq8", [128, 4], F32)
        bvb = alloc("bvb", [128, 512], F32)
        lgfg = alloc("lgfg", [128, D], F32)
        smalls = alloc("smalls", [128, 16], F32)
        uvt = alloc("uvt", [128, 8], F32)
        ssA = alloc("ssA", [128, NTILE], F32)
        rsA = alloc("rsA", [128, NTILE], F32)
        ssD = alloc("ssD", [128, 16], F32)
        rsD = alloc("rsD", [128, 16], F32)
        NWS = 4
        wsl = [alloc(f"wsl{i}", [128, 8, 128], BF16) for i in range(NWS)]
        regS = cur[0]
        u_ph = alloc("u_ph", [128, 16, UW], BF16)
        zc = alloc("zc", [128, 4, TM], BF16, at=OFF["u_ph"])
        wcv = alloc("wcv", [128, 144, 128], BF16)
        cvs = alloc("cvs", [128, 16, 512], BF16)
        sqs = [alloc(f"sqs{i}", [128, 512], BF16) for i in range(2)]
        pwr = alloc("pwr", [128, 16, 512], BF16)
        sgm = [alloc(f"sgm{i}", [128, 260], F32) for i in range(2)]
        f32t = [alloc(f"f32t{i}", [128, 512], F32) for i in range(8)]
        conv_end = cur[0]
        cur[0] = regS
        ttx = alloc("ttx", [128, 8, 896], BF16)
        rvt = alloc("rvt", [128, 8, 24], BF16)
        rwt = alloc("rwt", [128, 896], BF16)
        assert cur[0] <= OFF["wcv"]
        cur[0] = OFF["wcv"]
        vp = alloc("vp", [128, NTILE, 4, 128], BF16)
        wv = alloc("wv", [128, 8, 512], BF16)
        kT = alloc("kT", [128, NT], BF16)
        osq = [alloc(f"osq{i}", [128, 256], BF16) for i in range(3)]
        ess = [alloc(f"ess{i}", [128, 256], BF16) for i in range(3)]
        assert cur[0] <= OFF["cvs"], (cur[0], OFF["cvs"])
        cur[0] = OFF["cvs"]
        qzA = alloc("qzA", [128, TM], BF16)
        qzB = alloc("qzB", [128, TM], BF16)
        za = alloc("za", [128, TM], BF16)
        pT = [alloc(f"pT{i}", [128, 1536], BF16) for i in range(4)]
        ttw = alloc("ttw", [128, 2, 896], BF16)
        rvf = alloc("rvf", [128, 2, 1536], BF16)
        oS = alloc("oS", [128, TM], F32)
        stS = alloc("stS", [128, TM], F32)
        wo = alloc("wo", [128, 8, D], BF16)
        att_end = cur[0]
        pa0 = OFF["cvs"]
        NXS = 8
        xts = [alloc(f"xt{i}", [128, D], F32, at=OFF["wcv"] + i * 4096) for i in range(NXS)]
        xnb = [alloc(f"xnb{i}", [128, D], BF16, at=pa0 + i * 2048) for i in range(3)]
        junkA = alloc("junkA", [128, D], BF16, at=OFF["f32t0"])
        xrs = [alloc(f"xr{i}", [128, D], F32, at=OFF["vp"] + i * 4096) for i in range(3)]
        hts = [alloc(f"ht{i}", [128, D], F32, at=OFF["vp"] + 12288 + i * 4096) for i in range(3)]
        ots = [alloc(f"ot{i}", [128, D], F32, at=OFF["vp"] + 24576 + i * 4096) for i in range(2)]
        assert OFF["vp"] + 32768 <= OFF["osq0"]
        if os.environ.get("MK_MAP"):
            print("SBUF map: regS", regS, "conv_end", conv_end, "att_end", att_end, "top", top, "free", top - max(conv_end, att_end))
        assert max(conv_end, att_end) <= top, (conv_end, att_end, top)

        ps = st.enter_context(nc.psum_tensor("ps", [128, 4096], F32))

        def bank(b, w=512):
            return ps[:, b * 512:b * 512 + w]

        ident = cst[:, 0, :]
        bd32 = cst[:, 1, :]
        bd64s = cst[:, 2, :]

        def vcol(name, j=0, n=1):
            o, w = VC[name]
            return vecs[:, o + j:o + j + n]

        mhalf = smalls[:, 0:1]
        epsT = smalls[:, 1:2]

        def prog():
            last_bank_reader = [None] * 8

            def bank_dep(b):
                return last_bank_reader[b]

            e_cst = S.dma("sp", lambda e: e.dma_start(out=cst[:].rearrange("p a b -> p (a b)"), in_=cb_d), [], "c0")
            e_vec = S.dma("sp", lambda e: e.dma_start(out=vecs[:], in_=vec_d), [], "c0")
            e_lng = S.dma("sp", lambda e: e.dma_start(out=lgfg[:], in_=lng_d), [], "c0")
            e_bv = S.dma("sp", lambda e: e.dma_start(out=bvb[:], in_=bv_d), [], "c0")
            e_uv = S.dma("sp", lambda e: e.dma_start(out=uvt[:], in_=uv_d), [], "c0")
            C0 = e_uv
            e_m0 = S.op("pool", lambda e: e.memset(smalls[:], 0.0), [])
            e_m1 = S.op("pool", lambda e: e.memset(smalls[:, 0:1], -0.5), [e_m0])
            e_m2 = S.op("pool", lambda e: e.memset(smalls[:, 1:2], EPS), [e_m1])
            e_m3 = S.op("pool", lambda e: e.memset(ones64[:], 1.0), [e_m2])
            e_m4 = S.op("pool", lambda e: e.memset(ssA[:], 0.0), [e_m3])
            e_m5 = S.op("pool", lambda e: e.memset(ssD[:], 0.0), [e_m4])
            e_sm = e_m5
            e_bq8 = S.op("dve", lambda e: e.tensor_scalar(out=bq8[:], in0=vcol("bq", 0, 4), scalar1=0.125, scalar2=None, op0=ALU.mult), [C0])

            wslot_free = [None] * NWS
            wcount = [0]

            def load_wchunk(col0):
                i = wcount[0] % NWS
                wcount[0] += 1
                ev = S.dma("pool", lambda e, i=i, col0=col0: e.dma_start(
                    out=wsl[i][:], in_=win_d[:, col0:col0 + 128].rearrange("(k p) c -> p k c", p=128)),
                    [wslot_free[i]], f"w{i}")
                return i, ev

            chunk_cols = []
            for hh_ in range(2):
                for cc in range(4):
                    chunk_cols += [512 + cc * 128, cc * 128]
            chunk_cols += [1024 + cc * 128 for cc in range(4)]
            for pp in range(4):
                chunk_cols += [1536 + pp * 128, 2048 + pp * 128, 3072 + pp * 128]
            pending = []
            nxt = [0]

            in_use = [0]

            def prefetch():
                while len(pending) + in_use[0] < NWS and nxt[0] < len(chunk_cols):
                    pending.append(load_wchunk(chunk_cols[nxt[0]]))
                    nxt[0] += 1

            def take_chunk():
                if not pending:
                    prefetch()
                slot, ev = pending.pop(0)
                in_use[0] += 1
                return slot, ev

            def release_chunk(slot, pe_ev):
                wslot_free[slot] = pe_ev
                in_use[0] -= 1
                prefetch()

            SKIP = os.environ.get("MK_SKIP", "").split(",")
            if "wpre" not in SKIP:
                prefetch()
            def big_dma(q, dst2d, src2d, deps, sem, step=2048):
                n = dst2d.shape[1]
                evs = []
                for i_, c0 in enumerate(range(0, n, step)):
                    c1 = min(n, c0 + step)
                    evs.append(S.dma(q, lambda e, c0=c0, c1=c1: e.dma_start(out=dst2d[:, c0:c1], in_=src2d[:, c0:c1]), deps, f"{sem}{i_}"))
                return evs

            rot = [[2, 3, 4, 5, 6, 7]]
            rpos = [0]

            def next_bank():
                b = rot[0][rpos[0] % len(rot[0])]
                rpos[0] += 1
                return b

            e_e2 = {}
            u_evs = []
            b1_state = {"mm": 0, "ev": 0, "chunks": None, "info": {}}

            def b1_mm():
                n = b1_state["mm"]
                hh, rem = divmod(n, 16)
                cc, gi = divmod(rem, 4)
                G = cc * 4 + gi
                n0 = hh * 260
                hdep = [e_e2[9]] if hh == 0 else [e_e2[18]]
                if gi == 0:
                    sb_, eb_ = take_chunk()
                    sa_, ea_ = take_chunk()
                    b1_state["chunks"] = (sb_, eb_, sa_, ea_)
                sb_, eb_, sa_, ea_ = b1_state["chunks"]
                bb = next_bank()
                ba = next_bank()
                evs = {}
                for (bk_, sl_, ew_) in ((bb, sb_, eb_), (ba, sa_, ea_)):
                    evp = None
                    for k in range(8):
                        for g in range(4):
                            last = (k == 7 and g == 3)
                            evp = S.op("pe", lambda e, bk_=bk_, sl_=sl_, k=k, g=g, gi=gi, n0=n0: e.matmul(
                                ps[32 * g:32 * g + 32, bk_ * 512:bk_ * 512 + 260],
                                lhsT=wsl[sl_][:, k, gi * 32:gi * 32 + 32],
                                rhs=hnTp[:, k, g, n0:n0 + 260],
                                start=(k == 0), stop=(k == 7), tile_position=(0, 32 * g)),
                                [ew_, bank_dep(bk_)] + hdep, sig=last)
                    evs[bk_] = evp
                if gi == 3:
                    release_chunk(sb_, evs[ba])
                    release_chunk(sa_, evs[ba])
                b1_state["info"][n] = (G, n0, bb, ba, evs[bb], evs[ba])
                b1_state["mm"] = n + 1

            def b1_ev():
                n = b1_state["ev"]
                G, n0, bb, ba, ev_b, ev_a = b1_state["info"].pop(n)
                si = n % 2
                e_sg = S.op("act", lambda e, bb=bb, si=si, G=G: e.activation(
                    out=sgm[si][:], in_=ps[:, bb * 512:bb * 512 + 260], func=AF.Sigmoid, bias=vcol("bub", G), scale=1.0),
                    [ev_b, C0] + ([u_evs[-2]] if len(u_evs) >= 2 else []))
                last_bank_reader[bb] = e_sg
                e_u = S.op("dve", lambda e, ba=ba, si=si, G=G, n0=n0: e.scalar_tensor_tensor(
                    out=u_ph[:, G, n0:n0 + 260], in0=ps[:, ba * 512:ba * 512 + 260], scalar=vcol("bua", G),
                    in1=sgm[si][:], op0=ALU.add, op1=ALU.mult), [ev_a, e_sg, C0])
                last_bank_reader[ba] = e_u
                u_evs.append(e_u)
                b1_state["ev"] = n + 1

            def b1_step(limit, skew):
                if b1_state["mm"] < limit:
                    while b1_state["mm"] - b1_state["ev"] > skew:
                        b1_ev()
                    b1_mm()
                    if b1_state["mm"] - b1_state["ev"] > skew:
                        b1_ev()
                    return True
                return False

            NA = NTILE
            junkP = junkA[:]
            x_free = [None] * NXS
            xn_free = [None] * 3
            e_xl = [None] * NA
            e_pow = [None] * NA
            e_xnT = [None] * NA
            e_tp = [None] * NA
            e_e1 = [None] * NA
            evac_evs = []
            tp_bank = [0, 1]

            sq_prev = [None]

            def a_load(T):
                xs = T % NXS
                e_xl[T] = S.dma("sp", lambda e, T=T, xs=xs: e.dma_start(out=xts[xs][:], in_=x_d[T * 128:(T + 1) * 128, :]),
                                [x_free[xs]], f"x{xs}")

            def a_stat(T):
                xs = T % NXS
                e_sq = S.op("act", lambda e, T=T, xs=xs: e.activation(out=junkP, in_=xts[xs][:], func=AF.Square,
                                                                        accum_out=ssA[:, T:T + 1]), [e_xl[T], e_sm, sq_prev[0]])
                sq_prev[0] = e_sq
                e_r1 = S.op("dve", lambda e, T=T: e.tensor_scalar(out=rsA[:, T:T + 1], in0=ssA[:, T:T + 1], scalar1=1.0 / D,
                                                                   scalar2=EPS, op0=ALU.mult, op1=ALU.add), [e_sq])
                e_pow[T] = S.op("pool", lambda e, T=T: e.tensor_tensor(out=rsA[:, T:T + 1], in0=rsA[:, T:T + 1], in1=mhalf,
                                                                        op=ALU.pow), [e_r1, e_sm])

            def a_norm(T):
                xs = T % NXS
                nb = T % 3
                e_xnT[T] = S.op("dve", lambda e, T=T, xs=xs, nb=nb: e.scalar_tensor_tensor(
                    out=xnb[nb][:], in0=xts[xs][:], scalar=rsA[:, T:T + 1], in1=lgfg[:], op0=ALU.mult, op1=ALU.mult),
                    [e_pow[T], e_xl[T], C0, xn_free[nb]])
                x_free[xs] = e_xnT[T]
                b = tp_bank[T % 2]
                pb = bank(b).bitcast(BF16)
                ev = None
                for k in range(8):
                    ev = S.op("pe", lambda e, k=k, nb=nb, pb=pb: e.transpose(out=pb[:, k * 128:(k + 1) * 128],
                                                                             in_=xnb[nb][:, k * 128:(k + 1) * 128], identity=ident),
                              [e_xnT[T], C0, bank_dep(b)], sig=(k == 7))
                xn_free[nb] = ev
                e_tp[T] = ev

            def a_evac1(T):
                b = tp_bank[T % 2]
                pb = bank(b).bitcast(BF16)
                e_e1[T] = S.op("act", lambda e, T=T, pb=pb: e.activation(out=hnT[:, :, T * 128:(T + 1) * 128],
                                                                           in_=pb.rearrange("p (k t) -> p k t", k=8), func=AF.Copy), [e_tp[T]])
                last_bank_reader[b] = e_e1[T]
                evac_evs.append(e_e1[T])

            def a_evac2(T):
                if not (1 <= T <= 18):
                    return
                m_lo = 28 if T == 1 else 0
                m_hi = 4 if T == 18 else 32
                mm0 = 32 * T - 60 + m_lo
                nm = m_hi - m_lo
                src = hnT[:, :, T * 128:(T + 1) * 128].rearrange("p k (m g) -> p k g m", g=4)[:, :, :, m_lo:m_hi]
                ev2 = S.op("dve", lambda e, src=src, mm0=mm0, nm=nm: e.tensor_copy(out=hnTp[:, :, :, mm0:mm0 + nm], in_=src), [e_e1[T]])
                evac_evs.append(ev2)
                e_e2[T] = ev2

            PF = 5
            for T_ in range(PF):
                a_load(T_)
            for it in range(NA + 4):
                if it + PF < NA:
                    a_load(it + PF)
                if it < NA:
                    a_stat(it)
                if 0 <= it - 1 < NA:
                    a_norm(it - 1)
                if 0 <= it - 2 < NA:
                    a_evac1(it - 2)
                if 0 <= it - 3 < NA:
                    a_evac2(it - 3)
                if it - 3 >= 9:
                    for _ in range(2 if it % 2 == 0 else 1):
                        b1_step(16, 2)
            e_xn = e_xnT[NA - 1]
            A_DONE = list(evac_evs[-3:]) + [last_bank_reader[0], last_bank_reader[1]]
            e_hn_act = [e for e in evac_evs if e[0] == "act"][-1]
            e_hn_dve = ([e for e in evac_evs if e[0] == "dve"] or [None])[-1]
            HN = [e_hn_act, e_hn_dve]
            big_jobs = []
            wc2d = wcv[:].rearrange("p a b -> p (a b)")
            pw2d = pwr[:].rearrange("p a b -> p (a b)")
            e_wc = []
            e_pw = []
            for i_, c0_ in enumerate(range(0, 144 * 128, 2048)):
                big_jobs.append(lambda i_=i_, c0_=c0_: e_wc.append(S.dma("pool", lambda e, c0_=c0_: e.dma_start(out=wc2d[:, c0_:c0_ + 2048], in_=wc_d[:, c0_:c0_ + 2048]), [e_xn, sq_prev[0]], f"wc{i_}")))
            for i_, c0_ in enumerate(range(0, 16 * 512, 2048)):
                big_jobs.append(lambda i_=i_, c0_=c0_: e_pw.append(S.dma("pool", lambda e, c0_=c0_: e.dma_start(out=pw2d[:, c0_:c0_ + 2048], in_=pwr_d[:, c0_:c0_ + 2048]), [], f"pw{i_}")))
            if stage <= 1:
                while big_jobs:
                    big_jobs.pop(0)()
            yield 1, [x for x in HN + A_DONE + (e_wc or []) + (e_pw or []) if x is not None], {"hnT": hnT[:].rearrange("p a b -> p (a b)"), "hnTp": hnTp[:].rearrange("p a b c -> p (a b c)")}


            while b1_state["ev"] < b1_state["mm"]:
                b1_ev()
            rot[0] = [2, 3, 4, 5, 6, 7, 0, 1]
            rpos[0] = 0
            while b1_step(32, 2):
                if big_jobs:
                    big_jobs.pop(0)()
            while b1_state["ev"] < 32:
                b1_ev()
            while big_jobs:
                big_jobs.pop(0)()
            uedge = bass.AP(u_ph, 0, [[16 * UW, 128], [UW, 16], [516, 2], [1, 4]])
            uvb = bass.AP(uvt, 0, [[8, 128], [0, 16], [4, 2], [1, 4]])
            e_uv2 = S.op("dve", lambda e: e.tensor_tensor(out=uedge, in0=uedge, in1=uvb, op=ALU.mult), [u_evs[-1], C0])
            U_DONE = e_uv2
            yield 2, [U_DONE], {"u": u_ph[:].rearrange("p a b -> p (a b)")}

            b_sum = next_bank()
            b_sq = next_bank()
            conv_rot = [next_bank(), next_bank()]
            sq_free = [None, None]
            cv_evs = []
            st_last = None

            def b2_stats(G):
                qi = G % 2
                e_q = S.op("dve", lambda e, G=G, qi=qi: e.tensor_tensor(out=sqs[qi][:], in0=cvs[:, G, :], in1=cvs[:, G, :], op=ALU.mult),
                           [cv_evs[G], sq_free[qi]])
                S.op("pe", lambda e, G=G: e.matmul(bank(b_sum), lhsT=bd32, rhs=cvs[:, G, :], start=(G == 0), stop=(G == 15)),
                     [cv_evs[G], C0, bank_dep(b_sum)], sig=False)
                ev = S.op("pe", lambda e, G=G, qi=qi: e.matmul(bank(b_sq), lhsT=bd32, rhs=sqs[qi][:], start=(G == 0), stop=(G == 15)),
                          [e_q, bank_dep(b_sq)], sig=True)
                sq_free[qi] = ev
                return ev

            for G in range(16):
                bc = conv_rot[G % 2]
                for di in range(9):
                    evp = S.op("pe", lambda e, bc=bc, G=G, di=di: e.matmul(
                        bank(bc), lhsT=wcv[:, G * 9 + di, :], rhs=u_ph[:, G, di:di + 512], start=(di == 0), stop=(di == 8)),
                        [U_DONE, bank_dep(bc), e_xn, e_tp[NA - 1]] + e_wc, sig=(di == 8))
                e_cv = S.op("act", lambda e, bc=bc, G=G: e.activation(out=cvs[:, G, :], in_=bank(bc), func=AF.Identity,
                                                                        bias=vcol("dwb", G), scale=1.0), [evp, C0, e_tp[NA - 1]])
                last_bank_reader[bc] = e_cv
                cv_evs.append(e_cv)
                if G >= 1:
                    st_last = b2_stats(G - 1)
            st_last = b2_stats(15)
            mean, var, lnv, rstd, mr = f32t[0], f32t[1], f32t[2], f32t[3], f32t[4]
            e1 = S.op("dve", lambda e: e.tensor_scalar(out=mean[:], in0=bank(b_sum), scalar1=1.0 / 512, scalar2=None, op0=ALU.mult), [st_last])
            e2 = S.op("dve", lambda e: e.tensor_tensor(out=var[:], in0=mean[:], in1=mean[:], op=ALU.mult), [e1])
            e3 = S.op("dve", lambda e: e.scalar_tensor_tensor(out=var[:], in0=bank(b_sq), scalar=1.0 / 512, in1=var[:],
                                                               op0=ALU.mult, op1=ALU.subtract), [e2, st_last])
            last_bank_reader[b_sum] = e3
            last_bank_reader[b_sq] = e3
            e4 = S.op("act", lambda e: e.activation(out=lnv[:], in_=var[:], func=AF.Ln, bias=epsT, scale=1.0), [e3, e_sm])
            e5 = S.op("act", lambda e: e.activation(out=rstd[:], in_=lnv[:], func=AF.Exp, scale=-0.5), [e4])
            e6 = S.op("dve", lambda e: e.tensor_tensor(out=mr[:], in0=mean[:], in1=rstd[:], op=ALU.mult), [e5, e1])
            s_evs = []
            tfree = [None, None]
            zc_evs = []
            b4 = {"chunk": None}

            def b4_unit(cc, tt):
                if tt == 0:
                    b4["chunk"] = take_chunk()
                sl_, ew_ = b4["chunk"]
                b = next_bank()
                evp = None
                for k in range(8):
                    evp = S.op("pe", lambda e, b=b, sl_=sl_, k=k, tt=tt: e.matmul(
                        bank(b), lhsT=wsl[sl_][:, k, :], rhs=hnT[:, k, HALO + tt * 512:HALO + (tt + 1) * 512],
                        start=(k == 0), stop=(k == 7)), [ew_, bank_dep(b), U_DONE] + HN, sig=(k == 7))
                ez = S.op("act", lambda e, b=b, cc=cc, tt=tt: e.activation(out=zc[:, cc, tt * 512:(tt + 1) * 512], in_=bank(b),
                                                                             func=AF.Silu, bias=vcol("bzc", cc), scale=1.0),
                          [evp, C0, st_last])
                last_bank_reader[b] = ez
                zc_evs.append(ez)
                if tt == 3:
                    release_chunk(sl_, evp)

            for G in range(16):
                ti = G % 2
                tt_ = f32t[5 + ti]
                ea = S.op("dve", lambda e, G=G, tt_=tt_: e.tensor_tensor(out=tt_[:], in0=cvs[:, G, :], in1=rstd[:], op=ALU.mult),
                          [e5, cv_evs[G], st_last, tfree[ti]])
                eb = S.op("dve", lambda e, tt_=tt_: e.tensor_tensor(out=tt_[:], in0=tt_[:], in1=mr[:], op=ALU.subtract), [ea, e6])
                es = S.op("act", lambda e, G=G, tt_=tt_: e.activation(out=cvs[:, G, :], in_=tt_[:], func=AF.Silu,
                                                                        bias=vcol("clnb", G), scale=vcol("clng", G)), [eb, C0])
                tfree[ti] = es
                s_evs.append(es)
                b4_unit(G // 4, G % 4)
            S_DONE = s_evs[-1]
            yield 3, [S_DONE], {"s": cvs[:].rearrange("p a b -> p (a b)")}
            ZC_DONE = zc_evs[-1]

            pw_banks = [next_bank() for _ in range(4)]
            b_stats = [next_bank(), next_bank()]
            y_evs = []
            units = []

            def b5_front(u, cp, h, ev_pw):
                bh = pw_banks[h]
                si = u % 4
                qi = u % 2
                yc, rr = f32t[si * 2], f32t[si * 2 + 1]
                prev = units[u - 4]["ey"] if u >= 4 else None
                eyc = S.op("act", lambda e, bh=bh, yc=yc, cp=cp: e.activation(out=yc[:], in_=bank(bh), func=AF.Identity,
                                                                                bias=vcol("pwb", cp), scale=1.0), [ev_pw, C0, prev, e6, s_evs[-1]])
                last_bank_reader[bh] = eyc
                eys = S.op("pool", lambda e, yc=yc, qi=qi: e.tensor_tensor(out=sqs[qi][:], in0=yc[:], in1=yc[:], op=ALU.mult), [eyc, sq_free[qi]])
                bs = b_stats[qi]
                est = S.op("pe", lambda e, qi=qi, bs=bs: e.matmul(bank(bs), lhsT=bd64s, rhs=sqs[qi][:], start=True, stop=True),
                           [eys, C0, bank_dep(bs)])
                sq_free[qi] = est
                units.append({"cp": cp, "h": h, "eyc": eyc, "est": est, "si": si, "qi": qi})

            def b5_back(u):
                info = units[u]
                cp, h, si, qi = info["cp"], info["h"], info["si"], info["qi"]
                yc, rr = f32t[si * 2], f32t[si * 2 + 1]
                bs = b_stats[qi]
                el = S.op("act", lambda e, rr=rr, bs=bs: e.activation(out=rr[:], in_=bank(bs), func=AF.Ln, bias=epsT, scale=1.0), [info["est"]])
                last_bank_reader[bs] = el
                er = S.op("act", lambda e, rr=rr: e.activation(out=rr[:], in_=rr[:], func=AF.Exp, scale=-0.5), [el])
                et = S.op("dve", lambda e, yc=yc, rr=rr: e.tensor_tensor(out=yc[:], in0=yc[:], in1=rr[:], op=ALU.mult), [info["eyc"], er, info["est"]])
                zsrc = bass.AP(zc, cp * TM + h, [[4 * TM, 128], [4, 512]])
                ydst = bass.AP(yT, cp * TM + h, [[8 * TM, 128], [4, 512]])
                ey = S.op("dve", lambda e, yc=yc, cp=cp, zsrc=zsrc, ydst=ydst: e.scalar_tensor_tensor(
                    out=ydst, in0=yc[:], scalar=vcol("gcg", cp), in1=zsrc, op0=ALU.mult, op1=ALU.mult),
                    [et, ZC_DONE, C0, u_evs[-1]])
                info["ey"] = ey
                y_evs.append(ey)

            e_wv = S.dma("pool", lambda e: e.dma_start(out=wv[:], in_=win_d[:, 2560:3072].rearrange("(k p) c -> p k c", p=128)), [st_last], "wv")
            v_banks = [b for b in range(8) if b not in pw_banks and b not in b_stats]
            v_evs = []

            def v_unit(T):
                b = v_banks[T % len(v_banks)]
                evp = None
                for k in range(8):
                    evp = S.op("pe", lambda e, b=b, k=k, T=T: e.matmul(bank(b), lhsT=hnT[:, k, T * 128:(T + 1) * 128], rhs=wv[:, k, :],
                                                                       start=(k == 0), stop=(k == 7)), [e_wv, bank_dep(b), st_last] + HN, sig=(k == 7))
                evv = S.op("dve", lambda e, b=b, T=T: e.tensor_tensor(out=vp[:, T, :, :].rearrange("p a b -> p (a b)"), in0=bank(b), in1=bvb[:], op=ALU.add),
                           [evp, C0, st_last])
                last_bank_reader[b] = evv
                v_evs.append(evv)

            vT = [0]

            def v_some(n):
                for _ in range(n):
                    if vT[0] < NTILE:
                        v_unit(vT[0])
                        vT[0] += 1

            u = 0
            for cp in range(4):
                evh = [None] * 4
                for G in range(16):
                    for h in range(4):
                        evh[h] = S.op("pe", lambda e, cp=cp, G=G, h=h: e.matmul(
                            bank(pw_banks[h]), lhsT=pwr[32 * h:32 * h + 32, G, cp * 128:(cp + 1) * 128],
                            rhs=cvs[32 * h:32 * h + 32, G, :], start=(G == 0), stop=(G == 15), tile_position=(32 * h, 0)),
                            [S_DONE, bank_dep(pw_banks[h])] + e_pw, sig=(G == 15))
                for h in range(4):
                    b5_front(u, cp, h, evh[h])
                    if u >= 1:
                        b5_back(u - 1)
                    u += 1
                    v_some(1 if h < 3 else 2)
            b5_back(u - 1)
            v_some(NTILE)
            V_DONE = v_evs[-1]
            er = None
            YC_DONE = y_evs[-1]
            yield 4, [YC_DONE], {"yT": yT[:].rearrange("p a b -> p (a b)"), "zc": zc[:].rearrange("p a b -> p (a b)")}

            bar_pe = S.op("pe", lambda e: e.matmul(bank(b_stats[0])[:, 0:8], lhsT=ident, rhs=ident[:, 0:8], start=True, stop=True),
                          [YC_DONE, bank_dep(b_stats[0]), bank_dep(b_stats[1])])
            bar_act = S.op("act", lambda e: e.activation(out=smalls[:, 10:11], in_=smalls[:, 9:10], func=AF.Copy), [YC_DONE, bar_pe])
            bar_dve = S.op("dve", lambda e: e.tensor_copy(out=smalls[:, 11:12], in_=smalls[:, 9:10]), [bar_act, bar_pe])
            BAR = [bar_pe, bar_act, bar_dve]
            last_bank_reader[:] = [bar_dve] * 8

            e_tt = big_dma("pool", ttx[:].rearrange("p a b -> p (a b)"), ttx_d, BAR, "tt", step=1792)
            e_rv = S.dma("pool", lambda e: e.dma_start(out=rvt[:].rearrange("p a b -> p (a b)"), in_=rv_d), BAR, "rv")
            e_wos = []
            for nh_ in range(2):
                e_wo = S.dma("pool", lambda e, nh_=nh_: e.dma_start(out=wo[:, :, nh_ * 512:(nh_ + 1) * 512], in_=wout_d[:, nh_ * 512:(nh_ + 1) * 512].rearrange("(k p) c -> p k c", p=128)), BAR, f"wo{nh_}")
                e_wos.append(e_wo)
            e_tx = None
            e_rw = S.dma("pool", lambda e: e.dma_start(out=rwt[:], in_=rw_d), BAR, "rw")
            e_rvf = None
            for cls_, qb_ in ((0, 0), (1, 7)):
                e_rvf = S.op("dve", lambda e, cls_=cls_, qb_=qb_: e.tensor_copy(
                    out=rvf[:, cls_, :].rearrange("p (a c) -> p a c", a=24), in_=rvt[:, qb_, :].unsqueeze(2).to_broadcast([128, 24, 64])),
                    BAR + [e_rv, e_rvf])
            e_z1 = S.op("pool", lambda e: e.memset(qzA[64:128, :], 0.0), BAR)
            e_z2 = S.op("pool", lambda e: e.memset(qzB[0:64, :], 0.0), BAR + [e_z1])
            e_pz = e_z2
            for i_ in range(4):
                e_pz = S.op("pool", lambda e, i_=i_: e.memset(pT[i_][:, 0:192], 0.0), BAR + [e_pz])
                e_pz = S.op("pool", lambda e, i_=i_: e.memset(pT[i_][:, 1408:1536], 0.0), BAR + [e_pz])

            SB_A, SB_B, B_OS, B_ST = 0, 3, 6, 7
            pair_done = None
            pair_evs = []
            pT_free = [None, None, None, None]
            st_state = {"os_free": None, "st_free": None, "sq_free": [None, None, None]}
            tw_free = None
            sA_free = None
            sB_free = None
            os_free = None
            st_free = None
            att_tail = []
            for pp in range(int(os.environ.get('MK_NPP', 4))):
                sq_, eq_ = take_chunk()
                sk_, ek_ = take_chunk()
                sz_, ez_ = take_chunk()
                if os.environ.get("MK_C1", "") == "wonly" and pp >= 1:
                    release_chunk(sq_, None); release_chunk(sk_, None); release_chunk(sz_, None)
                    continue
                proj_banks = [0, 1, 2, 3, 4, 5]
                pbi = [0]

                def nb_():
                    b = proj_banks[pbi[0] % 6]
                    pbi[0] += 1
                    return b
                extra = [pair_done] if pair_done is not None else []
                q_evs = []
                NKQ = int(os.environ.get("MK_NKQ", 8)) if pp >= 1 else 8
                for tt in range(4 if NKQ == 8 or pp == 0 else 1):
                    b = nb_()
                    for k in range(NKQ):
                        evp = S.op("pe", lambda e, b=b, k=k, tt=tt, sq_=sq_: e.matmul(
                            bank(b), lhsT=wsl[sq_][:, k, :], rhs=hnT[:, k, HALO + tt * 512:HALO + (tt + 1) * 512],
                            start=(k == 0), stop=(k == 7)), [eq_, bank_dep(b)] + HN + BAR, sig=(k == NKQ - 1))
                    if os.environ.get("MK_C1", "") == "qafter" and pp >= 1:
                        continue
                    if "qev" in SKIP and pp >= 1:
                        last_bank_reader[b] = evp
                        q_evs.append(evp)
                        continue
                    e_qa = S.op("act", lambda e, b=b, tt=tt, pp=pp: e.activation(out=qzA[0:64, tt * 512:(tt + 1) * 512], in_=ps[0:64, b * 512:(b + 1) * 512],
                                                                                   func=AF.Identity, bias=bq8[0:64, pp:pp + 1], scale=0.125), [evp, e_bq8] + BAR + extra)
                    e_qb = S.op("act", lambda e, b=b, tt=tt, pp=pp: e.activation(out=qzB[64:128, tt * 512:(tt + 1) * 512], in_=ps[64:128, b * 512:(b + 1) * 512],
                                                                                   func=AF.Identity, bias=bq8[64:128, pp:pp + 1], scale=0.125), [evp, e_bq8, e_z2] + BAR + extra)
                    last_bank_reader[b] = e_qb
                    q_evs.append(e_qb)
                if os.environ.get("MK_C1", "") == "qafter" and pp >= 1:
                    break
                release_chunk(sq_, evp)
                k_evs = []
                for tt in range(5):
                    b = nb_()
                    for k in range(8):
                        evp = S.op("pe", lambda e, b=b, k=k, tt=tt, sk_=sk_: e.matmul(
                            bank(b), lhsT=wsl[sk_][:, k, :], rhs=hnT[:, k, tt * 512:(tt + 1) * 512],
                            start=(k == 0), stop=(k == 7)), [ek_, bank_dep(b)] + HN + BAR, sig=(k == 7))
                    if "kev" in SKIP and pp >= 1:
                        last_bank_reader[b] = evp
                        k_evs.append(evp)
                        continue
                    e_k = S.op("dve", lambda e, b=b, tt=tt, pp=pp: e.tensor_scalar(out=kT[:, tt * 512:(tt + 1) * 512], in0=bank(b), scalar1=vcol("bk", pp),
                                                                                     scalar2=None, op0=ALU.add), [evp, C0] + BAR + extra)
                    last_bank_reader[b] = e_k
                    k_evs.append(e_k)
                release_chunk(sk_, evp)
                e_tx = S.op("act", lambda e, pp=pp: e.activation(out=ttx[:, 2 * pp:2 * pp + 2, :], in_=ttx[:, 2 * pp:2 * pp + 2, :], func=AF.Exp), e_tt + BAR)
                QK = [q_evs[-1], q_evs[-2], k_evs[-1], V_DONE, e_tx, e_rv]
                if os.environ.get("MK_C1", "") == "projafter" and pp >= 1:
                    continue

                if os.environ.get("MK_C1", "") == "proj":
                    yield 6, QK, {"kT": kT[:], "qzA": qzA[:], "qzB": qzB[:], "ttx": ttx[:].rearrange("p a b -> p (a b)")}

                e_tw = None
                for a_ in range(2):
                    e_tw = S.op("dve", lambda e, a_=a_, pp=pp: e.tensor_tensor(out=ttw[:, a_, :], in0=ttx[:, 2 * pp + a_, :], in1=rwt[:], op=ALU.mult),
                                [e_tx, e_rw, tw_free, e_tw])
                blk = {}

                def att_S(qb):
                    interior = 1 <= qb <= 6
                    info = {"masks": [], "pi": []}
                    for hd, (SB, qz) in enumerate(((SB_A, qzA), (SB_B, qzB))):
                        for sl in range(6):
                            kt = 5 - sl
                            lo, hi = (0, 4)
                            if interior:
                                lo, hi = {0: (3, 4), 5: (0, 2)}.get(sl, (0, 4))
                            c0 = SB * 512 + sl * 256
                            evp = S.op("pe", lambda e, c0=c0, lo=lo, hi=hi, kt=kt, qb=qb, qz=qz: e.matmul(
                                ps[:, c0 + lo * 64:c0 + hi * 64],
                                lhsT=kT[:, (2 * qb + kt) * 128:(2 * qb + kt + 1) * 128], rhs=qz[:, qb * 256 + lo * 64:qb * 256 + hi * 64],
                                start=True, stop=True), QK + [bank_dep(SB), bank_dep(SB + 1), bank_dep(SB + 2)], sig=(sl == 5))
                        pi = (qb % 2) * 2 + hd
                        x0, x1 = (192, 1408) if interior else (0, 1536)
                        e_ex = S.op("act", lambda e, SB=SB, pi=pi, x0=x0, x1=x1: e.activation(
                            out=pT[pi][:, x0:x1], in_=ps[:, SB * 512 + x0:SB * 512 + x1], func=AF.Exp), [evp, pT_free[pi], e_pz])
                        for bb_ in range(3):
                            last_bank_reader[SB + bb_] = e_ex
                        info["last_ex"] = e_ex
                        h = 2 * pp + hd
                        pview = pT[pi][:].rearrange("p (a b c) -> p a b c", a=6, b=4)
                        if interior:
                            tview = bass.AP(ttw, hd * 896, [[2 * 896, 128], [128, 6], [64, 4], [1, 64]])
                            e_mk = S.op("dve", lambda e, pview=pview, tview=tview: e.tensor_tensor(out=pview, in0=pview, in1=tview, op=ALU.mult), [e_ex, e_tw])
                        else:
                            tview = bass.AP(ttx, h * 896, [[8 * 896, 128], [128, 6], [64, 4], [1, 64]])
                            e_m1_ = S.op("dve", lambda e, pview=pview, tview=tview: e.tensor_tensor(out=pview, in0=pview, in1=tview, op=ALU.mult), [e_ex, e_tx])
                            cls_ = 0 if qb == 0 else 1
                            e_mk = S.op("dve", lambda e, pi=pi, cls_=cls_: e.tensor_tensor(out=pT[pi][:], in0=pT[pi][:], in1=rvf[:, cls_, :], op=ALU.mult), [e_m1_, e_rvf])
                        info["masks"].append(e_mk)
                        info["pi"].append(pi)
                    blk[qb] = info

                def att_PV(qb):
                    nonlocal_os = st_state
                    interior = 1 <= qb <= 6
                    info = blk[qb]
                    (piA, piB) = info["pi"]
                    deps = info["masks"] + [V_DONE, st_state["os_free"], e_sm]
                    e_pv = None
                    o0 = B_OS * 512

                    def rng(sl):
                        return (0, 4)
                    for sl in range(6):
                        Tk = 2 * qb + 5 - sl
                        lo, hi = rng(sl)
                        first = (sl == 0)
                        lastk = (sl == 5)
                        S.op("pe", lambda e, sl=sl, Tk=Tk, lo=lo, hi=hi, first=first, lastk=lastk, pp=pp, piA=piA: e.matmul(
                            ps[0:64, o0 + lo * 64:o0 + hi * 64], lhsT=vp[:, Tk, pp, 0:64], rhs=pT[piA][:, sl * 256 + lo * 64:sl * 256 + hi * 64],
                            start=first, stop=lastk, tile_position=(0, 0)), deps, sig=False)
                        S.op("pe", lambda e, sl=sl, Tk=Tk, lo=lo, hi=hi, first=first, lastk=lastk, pp=pp, piB=piB: e.matmul(
                            ps[64:128, o0 + lo * 64:o0 + hi * 64], lhsT=vp[:, Tk, pp, 64:128], rhs=pT[piB][:, sl * 256 + lo * 64:sl * 256 + hi * 64],
                            start=first, stop=lastk, tile_position=(0, 64)), deps, sig=False)
                    for sl in range(6):
                        lo, hi = rng(sl)
                        lastk = (sl == 5)
                        S.op("pe", lambda e, sl=sl, lo=lo, hi=hi, lastk=lastk, piA=piA: e.matmul(
                            ps[0:64, o0 + 256 + lo * 64:o0 + 256 + hi * 64], lhsT=ones64[:], rhs=pT[piA][:, sl * 256 + lo * 64:sl * 256 + hi * 64],
                            start=(sl == 0), stop=lastk, tile_position=(0, 0)), deps, sig=False)
                        e_pv = S.op("pe", lambda e, sl=sl, lo=lo, hi=hi, lastk=lastk, piB=piB: e.matmul(
                            ps[64:128, o0 + 256 + lo * 64:o0 + 256 + hi * 64], lhsT=ones64[:], rhs=pT[piB][:, sl * 256 + lo * 64:sl * 256 + hi * 64],
                            start=(sl == 0), stop=lastk, tile_position=(0, 64)), deps, sig=lastk)
                    pT_free[piA] = e_pv
                    pT_free[piB] = e_pv
                    oi = qb % 3
                    e_s2 = S.op("act", lambda e, oi=oi: e.activation(out=ess[oi][:], in_=ps[:, B_OS * 512 + 256:B_OS * 512 + 512], func=AF.Square, scale=1e-3),
                                [e_pv, st_state["sq_free"][oi]])
                    e_oc = S.op("dve", lambda e, qb=qb: e.tensor_copy(out=oS[:, qb * 256:(qb + 1) * 256], in_=ps[:, B_OS * 512:B_OS * 512 + 256]),
                                [e_pv, pair_done, e_s2])
                    st_state["os_free"] = e_oc
                    e_o2 = S.op("dve", lambda e, qb=qb, oi=oi: e.tensor_tensor(out=osq[oi][:], in0=oS[:, qb * 256:(qb + 1) * 256], in1=oS[:, qb * 256:(qb + 1) * 256], op=ALU.mult),
                                [e_oc, st_state["sq_free"][oi]])
                    info["e_s2"] = e_s2
                    info["e_o2"] = e_o2
                    info["e_oc"] = e_oc

                def att_stat(qb):
                    info = blk[qb]
                    oi = qb % 3
                    S.op("pe", lambda e, oi=oi: e.matmul(ps[:, B_ST * 512:B_ST * 512 + 256], lhsT=bd64s, rhs=osq[oi][:], start=True, stop=False),
                         [info["e_o2"], C0, st_state["st_free"]], sig=False)
                    e_stm = S.op("pe", lambda e, oi=oi: e.matmul(ps[:, B_ST * 512:B_ST * 512 + 256], lhsT=ident, rhs=ess[oi][:], start=False, stop=True),
                                 [info["e_s2"], C0], sig=True)
                    st_state["sq_free"][oi] = e_stm
                    info["e_stm"] = e_stm

                def att_stcopy(qb):
                    info = blk[qb]
                    ev = S.op("act", lambda e, qb=qb: e.activation(out=stS[:, qb * 256:(qb + 1) * 256], in_=ps[:, B_ST * 512:B_ST * 512 + 256], func=AF.Copy),
                              [info["e_stm"], pair_done])
                    st_state["st_free"] = ev
                    return ev

                NQ = 8
                att_S(0)
                last_sc = None
                for n in range(NQ):
                    if n >= 2:
                        last_sc = att_stcopy(n - 2)
                    if n + 1 < NQ:
                        att_S(n + 1)
                    att_PV(n)
                    if n >= 1:
                        att_stat(n - 1)
                if NQ >= 2:
                    last_sc = att_stcopy(NQ - 2)
                att_stat(NQ - 1)
                last_sc = att_stcopy(NQ - 1)
                e_ex = blk[NQ - 1]["last_ex"]
                z_evs = []
                for tt in range(4):
                    b = nb_()
                    for k in range(8):
                        evp = S.op("pe", lambda e, b=b, k=k, tt=tt, sz_=sz_: e.matmul(
                            bank(b), lhsT=wsl[sz_][:, k, :], rhs=hnT[:, k, HALO + tt * 512:HALO + (tt + 1) * 512],
                            start=(k == 0), stop=(k == 7)), [ez_, bank_dep(b)] + HN + BAR, sig=(k == 7))
                    if "zev" in SKIP and pp >= 1:
                        last_bank_reader[b] = evp
                        z_evs.append(evp)
                        continue
                    e_z = S.op("act", lambda e, b=b, tt=tt, pp=pp: e.activation(out=za[:, tt * 512:(tt + 1) * 512], in_=bank(b), func=AF.Silu,
                                                                                  bias=vcol("bza", pp), scale=1.0), [evp, C0] + BAR + extra)
                    last_bank_reader[b] = e_z
                    z_evs.append(e_z)
                release_chunk(sz_, evp)
                e_l = S.op("act", lambda e: e.activation(out=stS[:], in_=stS[:], func=AF.Ln), [last_sc])
                e_r = S.op("act", lambda e: e.activation(out=stS[:], in_=stS[:], func=AF.Exp, scale=-0.5), [e_l])
                e_t = S.op("dve", lambda e: e.tensor_tensor(out=oS[:], in0=oS[:], in1=stS[:], op=ALU.mult), [e_r, blk[NQ - 1]["e_oc"], blk[NQ - 1]["e_o2"]])
                e_y = S.op("dve", lambda e, pp=pp: e.scalar_tensor_tensor(out=yT[:, 4 + pp, :], in0=oS[:], scalar=vcol("gag", pp), in1=za[:],
                                                                           op0=ALU.mult, op1=ALU.mult), [e_t, z_evs[-1], C0])
                for _d in range(int(os.environ.get("MK_PAD", 0))):
                    e_y = S.op("dve", lambda e: e.tensor_copy(out=smalls[:, 12:13], in_=smalls[:, 9:10]), [e_y])
                pair_done = e_y
                pair_evs.append(e_y)
                tw_free = e_y
            ATT_DONE = pair_done
            yield 6, [ATT_DONE], {"yT": yT[:].rearrange("p a b -> p (a b)"), "oS": oS[:], "stS": stS[:], "za": za[:]}

            N_EARLY = 3 if len(pair_evs) == 4 else 0
            for T_ in range(N_EARLY):
                b0_ = (0, 2, 4)[T_]
                for nh_ in range(2):
                    for k_ in range(7):
                        S.op("pe", lambda e, b0_=b0_, nh_=nh_, k_=k_, T_=T_: e.matmul(
                            bank(b0_ + nh_), lhsT=yT[:, k_, T_ * 128:(T_ + 1) * 128], rhs=wo[:, k_, nh_ * 512:(nh_ + 1) * 512],
                            start=(k_ == 0), stop=False), [pair_evs[2], YC_DONE, bank_dep(b0_ + nh_)] + e_wos, sig=False)
            e_fg = S.dma("sp", lambda e: e.dma_start(out=lgfg[:], in_=fg_d), [ATT_DONE, e_xn], "fg")
            barD_pe = S.op("pe", lambda e: e.matmul(bank(7)[:, 0:8], lhsT=ident, rhs=ident[:, 0:8], start=True, stop=True), [ATT_DONE, st_free, os_free, e_ex])
            barD_act = S.op("act", lambda e: e.activation(out=smalls[:, 10:11], in_=smalls[:, 9:10], func=AF.Copy), [ATT_DONE, barD_pe])
            barD = [ATT_DONE, barD_pe, barD_act]
            junkDP = ps[:, 6 * 512:8 * 512]
            xr_free = [None, None, None]
            ht_free = [None, None, None]
            ot_free = [None, None]
            dbanks = [(0, 1), (2, 3), (4, 5)]
            d_free = [barD_act] * 3
            out_evs = []
            e_h = [None] * 16
            e_ssq = [None] * 16
            e_pw2 = [None] * 16

            def d_main(T):
                i2 = T % 2
                i3 = T % 3
                e_xr = S.dma("sp", lambda e, T=T, i3=i3: e.dma_start(out=xrs[i3][:], in_=x_d[HALO + T * 128:HALO + (T + 1) * 128, :]),
                             barD + [xr_free[i3]], f"xr{i3}")
                bi = T % 3
                b0 = dbanks[bi][0]
                evp = None
                for nh in range(2):
                    for k in range(7 if T < N_EARLY else 0, 8):
                        evp = S.op("pe", lambda e, b0=b0, nh=nh, k=k, T=T: e.matmul(
                            bank(b0 + nh), lhsT=yT[:, k, T * 128:(T + 1) * 128], rhs=wo[:, k, nh * 512:(nh + 1) * 512],
                            start=(k == 0), stop=(k == 7)), barD + e_wos + [d_free[bi], YC_DONE], sig=(k == 7 and nh == 1))
                e_h[T] = S.op("dve", lambda e, b0=b0, i2=i2, i3=i3: e.tensor_tensor(out=hts[i3][:], in0=ps[:, b0 * 512:b0 * 512 + 1024], in1=xrs[i3][:], op=ALU.add),
                              [evp, e_xr, ht_free[i3]])
                d_free[bi] = e_h[T]
                xr_free[i3] = e_h[T]
                e_ssq[T] = S.op("act", lambda e, T=T, i3=i3: e.activation(out=junkDP, in_=hts[i3][:], func=AF.Square, accum_out=ssD[:, T:T + 1]), [e_h[T], barD_act, e_ssq[T - 1] if T >= 1 else None])

            def d_rs(T):
                e_r1 = S.op("dve", lambda e, T=T: e.tensor_scalar(out=rsD[:, T:T + 1], in0=ssD[:, T:T + 1], scalar1=1.0 / D, scalar2=EPS,
                                                                   op0=ALU.mult, op1=ALU.add), [e_ssq[T]])
                e_pw2[T] = S.op("pool", lambda e, T=T: e.tensor_tensor(out=rsD[:, T:T + 1], in0=rsD[:, T:T + 1], in1=mhalf, op=ALU.pow), [e_r1])

            def d_out(T):
                i2 = T % 2
                i3 = T % 3
                e_o = S.op("dve", lambda e, T=T, i2=i2, i3=i3: e.scalar_tensor_tensor(out=ots[i2][:], in0=hts[i3][:], scalar=rsD[:, T:T + 1], in1=lgfg[:],
                                                                                       op0=ALU.mult, op1=ALU.mult), [e_pw2[T], e_fg, ot_free[i2]])
                ht_free[i3] = e_o
                e_st = S.dma("act", lambda e, T=T, i2=i2: e.dma_start(out=out_d[T * 128:(T + 1) * 128, :], in_=ots[i2][:]), [e_o], f"o{i2}")
                ot_free[i2] = e_st
                out_evs.append(e_st)

            for it in range(16 + 2):
                if it < 16:
                    d_main(it)
                if 0 <= it - 1 < 16:
                    d_rs(it - 1)
                if 0 <= it - 2 < 16:
                    d_out(it - 2)
            fin = out_evs[-2:]
            yield 99, fin, {}

        final = None
        for stg, fdeps, dumps in prog():
            if stg >= stage:
                final = (fdeps, dumps)
                break
        fdeps, dumps = final
        fin = list(fdeps)
        for nm, src in dumps.items():
            dd_ = nc.dram_tensor('dbg_' + nm, [128, int(np.prod(src.shape[1:]))], src.dtype, kind='ExternalOutput').ap()
            n_ = dd_.shape[1]
            for c0 in range(0, n_, 2048):
                c1 = min(n_, c0 + 2048)
                ev_ = S.dma('sp', lambda e, dd_=dd_, src=src, c0=c0, c1=c1: e.dma_start(out=dd_[:, c0:c1], in_=src[:, c0:c1]), list(fdeps), 'dbg_' + nm)
            fin.append(ev_)
        S.wait("sp", fin)
        S.emit(nc, st)
    return nc


def _host_consts():
    ident = np.eye(128, dtype=np.float32)
    p = np.arange(128)
    bd32 = (p[:, None] // 32 == p[None, :] // 32).astype(np.float32)
    bd64s = (p[:, None] // 64 == p[None, :] // 64).astype(np.float32) / 64.0
    return np.concatenate([ident, bd32, bd64s], axis=1).astype(ml_dtypes.bfloat16)


def _ph(vec512):
    a = np.asarray(vec512, np.float32).reshape(16, 32)
    return np.tile(a.T, (4, 1))


def _cm(vec, n):
    return np.ascontiguousarray(np.asarray(vec, np.float32).reshape(n, 128).T)


def _shared_layouts(inp):
    w_in = np.ascontiguousarray(inp["w_in"][0], dtype=np.float32)
    b_in = np.asarray(inp["b_in"][0], np.float32)
    vecs = np.zeros((128, NVEC), np.float32)

    def put(name, arr):
        o, w = VC[name]
        assert arr.shape == (128, w), (name, arr.shape)
        vecs[:, o:o + w] = arr
    put("bua", _ph(b_in[0:512]))
    put("bub", _ph(b_in[512:1024]))
    put("bzc", _cm(b_in[1024:1536], 4))
    put("bq", _cm(b_in[1536:2048], 4))
    put("bk", _cm(b_in[2048:2560], 4))
    put("bza", _cm(b_in[3072:3584], 4))
    put("dwb", _ph(inp["dw_b"][0]))
    put("clng", _ph(inp["cln_g"][0]))
    put("clnb", _ph(inp["cln_b"][0]))
    put("pwb", _cm(inp["pw_b"][0], 4))
    put("gcg", _cm(inp["gn_conv_g"][0], 4))
    put("gag", _cm(inp["gn_att_g"][0], 4))
    bv_b = np.ascontiguousarray(np.broadcast_to(b_in[2560:3072][None, :], (128, 512)), dtype=np.float32)
    lng_b = np.ascontiguousarray(np.broadcast_to(np.asarray(inp["ln_g"][0], np.float32)[None, :], (128, D)))
    fg_b = np.ascontiguousarray(np.broadcast_to(np.asarray(inp["final_g"], np.float32)[None, :], (128, D)))
    pw = np.asarray(inp["pw_w"][0], np.float32).reshape(16, 32, 512)
    pw_rep = np.tile(pw.transpose(1, 0, 2), (4, 1, 1)).reshape(128, 16 * 512)
    dw = np.asarray(inp["dw_w"][0], np.float32)
    wc = np.zeros((4, 32, 16, 9, 4, 32), np.float32)
    cidx = np.arange(32)
    for g in range(4):
        for h in range(4):
            for di in range(9):
                j = 4 * (di - 4) + g - h + 15
                if 0 <= j <= 30:
                    for G in range(16):
                        wc[g, cidx, G, di, h, cidx] = dw[j, G * 32:(G + 1) * 32]
    wconv = np.ascontiguousarray(wc.reshape(128, 144 * 128))
    rpb = np.asarray(inp["rpb"][0], np.float32)
    c = np.arange(64)
    w = np.arange(64)
    cs = np.clip(w - 8, 0, 48)
    colok = (c[:, None] >= cs[None, :]) & (c[:, None] < cs[None, :] + 16)
    crel = np.clip(c[:, None] - w[None, :] + 15, 0, 30)
    ttxb = np.full((2, 64, 8, 14, 64), NEG, np.float32)
    for i2 in range(2):
        for e in range(14):
            delta = 6 - e + i2
            for h in range(8):
                vals = rpb[h, delta + 7][crel]
                ttxb[i2, :, h, e, :] = np.where(colok, vals, NEG)
    ttxb = np.ascontiguousarray(ttxb.reshape(128, 8 * 896))
    rwin = np.zeros((2, 64, 14), np.float32)
    for i2 in range(2):
        for e in range(14):
            if -4 <= 6 - e + i2 <= 3:
                rwin[i2, :, e] = 1.0
    rwin = np.ascontiguousarray(np.repeat(rwin.reshape(128, 14, 1), 64, axis=2).reshape(128, 896))
    return dict(w_in=w_in, w_out=np.ascontiguousarray(inp["w_out"][0], dtype=np.float32), pw_rep=np.ascontiguousarray(pw_rep),
                wconv=wconv, constb=_host_consts(), vecs=vecs, bv_b=bv_b, lng_b=lng_b, fg_b=fg_b, ttxb=ttxb, rwin=rwin)


def _core_layouts(inp, core):
    b, j = core // 4, core % 4
    x = np.asarray(inp["x"], np.float32)
    lo = j * TM - HALO
    x_ext = np.zeros((NT, D), np.float32)
    s0, s1 = max(lo, 0), min(lo + NT, SEQ)
    x_ext[s0 - lo:s1 - lo] = x[b, s0:s1]
    R0 = 32 * j
    rv = np.zeros((2, 64, 8, 6, 4), np.float32)
    for qb in range(8):
        for sl in range(6):
            kt = 5 - sl
            for r4 in range(4):
                r = R0 + 4 * qb + r4
                rs = min(max(r - 4, 0), 120)
                for i2 in range(2):
                    kr = R0 - 4 + 4 * qb + 2 * kt + i2
                    if rs <= kr < rs + 8:
                        rv[i2, :, qb, sl, r4] = 1.0
    rv = np.ascontiguousarray(rv.reshape(128, 8 * 24))
    uvalid = np.ones((128, 8), np.float32)
    if j == 0:
        uvalid[:, 0:4] = 0.0
    if j == 3:
        uvalid[:, 4:8] = 0.0
    return dict(x_ext=x_ext, rv=rv, uvalid=uvalid)


_NC_CACHE = {}


def kernel(**inputs):
    shared = _shared_layouts(inputs)
    in_maps = []
    for core in range(NCORE):
        m = dict(shared)
        m.update(_core_layouts(inputs, core))
        in_maps.append(m)
    if DEBUG not in _NC_CACHE:
        _NC_CACHE[DEBUG] = build_nc(debug=DEBUG)
    nc = _NC_CACHE[DEBUG]
    res = run_bass_kernel_spmd(nc, in_maps, core_ids=list(range(NCORE)))
    out = np.empty((2, SEQ, D), np.float32)
    for core in range(NCORE):
        b, j = core // 4, core % 4
        out[b, j * TM:(j + 1) * TM] = res.results[core]["out"]
    if DEBUG:
        kernel.last_results = res.results
    return out
```
